# Optimizing a Trainium2 kernel written in Bass

```python
import jax, jax.numpy as jnp
from jax import lax
import numpy as np

D_MODEL = 2048
BATCH = 8
SEQ = 2048
DEPTH = 2
DEC_BATCH = 16
DEC_SEQ = 2048
PAST_LEN = 128

GRID_W = 64
EPS = 1e-6
N_BRANCH = 3
W_A = D_MODEL // 2
CONV_A = 3
H_B = 8
DK_B = 128
DV_B = 128
W_B = H_B * DV_B
CONV_B = 3
CHUNK = 64
H_C = 8
D_C = 128
W_C = H_C * D_C
MAX_WIN_H = 8
WIN_W = 16
QB_W = 16
KB_W = QB_W + WIN_W
OFF_A = 0
OFF_B = OFF_A + 4 * W_A
OFF_C = OFF_B + 4 * W_B + 4 * H_B
OFF_G = OFF_C + 4 * W_C
N_IN = OFF_G + N_BRANCH * D_MODEL

kernel_name = 'hybrid_bidir_conv_gdn_natten_encoder'


def rms_norm(x, w):
    xf = x.astype(jnp.float32)
    y = xf * lax.rsqrt(jnp.mean(xf * xf, axis=-1, keepdims=True) + EPS)
    return (y * w.astype(jnp.float32)).astype(x.dtype)


def l2norm(t):
    return t * lax.rsqrt(jnp.sum(t * t, axis=-1, keepdims=True) + EPS)


def centred_dwconv(u, w):
    k_w = w.shape[0]
    p = k_w // 2
    s = u.shape[1]
    up = jnp.pad(u, ((0, 0), (p, p), (0, 0)))
    out = w[0] * up[:, 0:s]
    for i in range(1, k_w):
        out = out + w[i] * up[:, i:i + s]
    return out


def gated_delta_chunked(q, k, v, g, beta):
    b, s, H, dk = q.shape
    dv = v.shape[-1]
    n = s // CHUNK

    def chunks(t):
        return t.reshape((b, n, CHUNK, H) + t.shape[3:]).swapaxes(2, 3)

    qc, kc, vc, gc, bc = chunks(q), chunks(k), chunks(v), chunks(g), chunks(beta)
    gc = jnp.cumsum(gc, axis=-1)
    idx = jnp.arange(CHUNK)
    tril = idx[:, None] >= idx[None, :]
    strict = idx[:, None] > idx[None, :]
    decay = jnp.exp(jnp.where(tril, gc[..., :, None] - gc[..., None, :], -jnp.inf))
    kbeta = kc * bc[..., None]
    lower = jnp.where(strict, jnp.einsum('bnhid,bnhjd->bnhij', kbeta, kc) * decay, 0.0)
    a_mat = lower + jnp.eye(CHUNK, dtype=lower.dtype)
    rhs = jnp.concatenate([vc * bc[..., None], kbeta * jnp.exp(gc)[..., None]], axis=-1)
    sol = lax.linalg.triangular_solve(a_mat, rhs, left_side=True, lower=True, unit_diagonal=True)
    u_c, w_c = sol[..., :dv], sol[..., dv:]
    attn_intra = jnp.where(tril, jnp.einsum('bnhid,bnhjd->bnhij', qc, kc) * decay, 0.0)
    g_last = gc[..., -1]
    k_dec = kc * jnp.exp(g_last[..., None] - gc)[..., None]
    q_dec = qc * jnp.exp(gc)[..., None]

    def step(state, inp):
        q_i, k_i, u_i, w_i, a_i, gl_i = inp
        v_new = u_i - jnp.einsum('bhck,bhkv->bhcv', w_i, state)
        o_i = jnp.einsum('bhck,bhkv->bhcv', q_i, state) + jnp.einsum('bhij,bhjv->bhiv', a_i, v_new)
        state = state * jnp.exp(gl_i)[..., None, None] + jnp.einsum('bhck,bhcv->bhkv', k_i, v_new)
        return state, o_i

    mv = lambda t: jnp.moveaxis(t, 1, 0)
    s0 = jnp.zeros((b, H, dk, dv), q.dtype)
    _, o = lax.scan(step, s0, (mv(q_dec), mv(k_dec), mv(u_c), mv(w_c), mv(attn_intra), mv(g_last)))
    return o.transpose(1, 0, 3, 2, 4).reshape(b, s, H, dv)


def bidir_gdn(q, k, v, g_f, g_b, beta_f, beta_b):
    b = q.shape[0]
    flip = lambda t: jnp.flip(t, axis=1)
    cat = lambda t_f, t_b: jnp.concatenate([t_f, flip(t_b)], axis=0)
    o = gated_delta_chunked(cat(q, q), cat(k, k), cat(v, v), cat(g_f, g_b), cat(beta_f, beta_b))
    return o[:b] + flip(o[b:])


def neighbourhood_attention(q, k, v, bias_table):
    b, s, H, d = q.shape
    rows = s // GRID_W
    win_h = min(MAX_WIN_H, rows)
    nqb = GRID_W // QB_W
    c0 = np.arange(nqb) * QB_W
    kc0 = np.clip(c0 - WIN_W // 2, 0, GRID_W - KB_W)
    qcol = c0[:, None] + np.arange(QB_W)
    kcol = kc0[:, None] + np.arange(KB_W)
    cs = np.clip(qcol - WIN_W // 2, 0, GRID_W - WIN_W)
    valid = (kcol[:, None, :] >= cs[:, :, None]) & (kcol[:, None, :] < cs[:, :, None] + WIN_W)
    dc_idx = np.clip(kcol[:, None, :] - qcol[:, :, None] + WIN_W - 1, 0, 2 * WIN_W - 2)
    col_bias = bias_table[:, :, dc_idx]
    kg = k.reshape(b, rows, GRID_W, H, d)
    vg = v.reshape(b, rows, GRID_W, H, d)
    qr = q.reshape(b, rows, nqb, QB_W, H, d).transpose(1, 0, 2, 3, 4, 5)
    valid_b = jnp.asarray(valid[:, :, None, :])

    def one_row(args):
        q_row, r = args
        rs = jnp.clip(r - win_h // 2, 0, rows - win_h)
        kb = lax.dynamic_slice_in_dim(kg, rs, win_h, axis=1)[:, :, kcol]
        vb = lax.dynamic_slice_in_dim(vg, rs, win_h, axis=1)[:, :, kcol]
        sc = jnp.einsum('bnihd,bwnjhd->bhniwj', q_row, kb).astype(jnp.float32)
        dr_idx = rs + jnp.arange(win_h) - r + MAX_WIN_H - 1
        rb = jnp.take(col_bias, dr_idx, axis=1).transpose(0, 2, 3, 1, 4)
        sc = jnp.where(valid_b, sc + rb.astype(jnp.float32)[None], -jnp.inf)
        p = jax.nn.softmax(sc, axis=(-2, -1))
        o = jnp.einsum('bhniwj,bwnjhd->bnihd', p.astype(v.dtype), vb)
        return o.reshape(b, GRID_W, H, d)

    o = lax.map(one_row, (qr, jnp.arange(rows)))
    return o.transpose(1, 0, 2, 3, 4).reshape(b, s, H, d)


def mixer_layer(x, nw, w_in, conv_a, conv_b, a_log, dt_bias, gdn_nw, na_bias, w_pa, w_pb, w_pc, w_o):
    b, s, _ = x.shape
    f32 = jnp.float32
    h = rms_norm(x, nw)
    proj = jnp.einsum('bsd,de->bse', h, w_in)
    xa, ba, ca, ga = jnp.split(proj[..., OFF_A:OFF_B], 4, axis=-1)
    y_a = ba * centred_dwconv(ca * xa, conv_a) * jax.nn.silu(ga)
    pb = proj[..., OFF_B:OFF_C]
    qkv = jax.nn.silu(centred_dwconv(pb[..., :3 * W_B], conv_b))
    q_b, k_b, v_b = [t.reshape(b, s, H_B, -1).astype(f32) for t in jnp.split(qkv, 3, axis=-1)]
    gate_b = pb[..., 3 * W_B:4 * W_B]
    beta = jax.nn.sigmoid(pb[..., 4 * W_B:4 * W_B + 2 * H_B].astype(f32)).reshape(b, s, 2, H_B)
    alpha = pb[..., 4 * W_B + 2 * H_B:].astype(f32).reshape(b, s, 2, H_B)
    g = -jnp.exp(a_log.astype(f32)) * jax.nn.softplus(alpha + dt_bias.astype(f32))
    q_b = l2norm(q_b) * (DK_B ** -0.5)
    k_b = l2norm(k_b)
    o_b = bidir_gdn(q_b, k_b, v_b, g[:, :, 0], g[:, :, 1], beta[:, :, 0], beta[:, :, 1])
    o_b = rms_norm(o_b, gdn_nw)
    y_b = o_b.reshape(b, s, W_B).astype(x.dtype) * jax.nn.silu(gate_b)
    q_c, k_c, v_c, gate_c = jnp.split(proj[..., OFF_C:OFF_G], 4, axis=-1)
    o_c = neighbourhood_attention(q_c.reshape(b, s, H_C, D_C) * (D_C ** -0.5),
                                  k_c.reshape(b, s, H_C, D_C), v_c.reshape(b, s, H_C, D_C), na_bias)
    y_c = o_c.reshape(b, s, W_C) * jax.nn.silu(gate_c)
    gates = jax.nn.sigmoid(proj[..., OFF_G:]).reshape(b, s, N_BRANCH, D_MODEL)
    merged = (gates[:, :, 0] * (y_a @ w_pa) + gates[:, :, 1] * (y_b @ w_pb)
              + gates[:, :, 2] * (y_c @ w_pc))
    return x + merged @ w_o


def setup_inputs(seed: int = 0) -> dict:
    key = jax.random.key(seed)
    ks = jax.random.split(key, 16)
    nrm = jax.random.normal
    dt = jnp.exp(jax.random.uniform(ks[6], (DEPTH, 2, H_B), minval=np.log(1e-3), maxval=np.log(1e-1)))
    return {
        'x_prompt': nrm(ks[0], (BATCH, SEQ, D_MODEL), jnp.float32),
        'x_sample': nrm(ks[1], (DEC_BATCH, DEC_SEQ, D_MODEL), jnp.float32),
        'norm_w': 1.0 + 0.02 * nrm(ks[2], (DEPTH, D_MODEL), jnp.float32),
        'w_in': nrm(ks[3], (DEPTH, D_MODEL, N_IN), jnp.float32) * D_MODEL ** -0.5,
        'conv_a': nrm(ks[4], (DEPTH, CONV_A, W_A), jnp.float32) * CONV_A ** -0.5,
        'conv_b': nrm(ks[5], (DEPTH, CONV_B, 3 * W_B), jnp.float32) * CONV_B ** -0.5,
        'a_log': jnp.log(jax.random.uniform(ks[7], (DEPTH, 2, H_B), minval=1.0, maxval=16.0)),
        'dt_bias': dt + jnp.log(-jnp.expm1(-dt)),
        'gdn_norm_w': 1.0 + 0.02 * nrm(ks[8], (DEPTH, DV_B), jnp.float32),
        'na_bias': 0.02 * nrm(ks[9], (DEPTH, H_C, 2 * MAX_WIN_H - 1, 2 * WIN_W - 1), jnp.float32),
        'w_pa': nrm(ks[10], (DEPTH, W_A, D_MODEL), jnp.float32) * W_A ** -0.5,
        'w_pb': nrm(ks[11], (DEPTH, W_B, D_MODEL), jnp.float32) * W_B ** -0.5,
        'w_pc': nrm(ks[12], (DEPTH, W_C, D_MODEL), jnp.float32) * W_C ** -0.5,
        'w_o': nrm(ks[13], (DEPTH, D_MODEL, D_MODEL), jnp.float32) * D_MODEL ** -0.5,
        'final_norm_w': 1.0 + 0.02 * nrm(ks[14], (D_MODEL,), jnp.float32),
    }


def reference(x_prompt, x_sample, norm_w, w_in, conv_a, conv_b, a_log, dt_bias, gdn_norm_w,
              na_bias, w_pa, w_pb, w_pc, w_o, final_norm_w):
    def trunk(x):
        for l in range(DEPTH):
            x = mixer_layer(x, norm_w[l], w_in[l], conv_a[l], conv_b[l], a_log[l], dt_bias[l],
                            gdn_norm_w[l], na_bias[l], w_pa[l], w_pb[l], w_pc[l], w_o[l])
        return rms_norm(x, final_norm_w)

    y_prompt = trunk(x_prompt)
    y_sample = trunk(x_sample)
    return (y_prompt, y_sample)
```

```python
import numpy as np
import concourse.bass as bass
import concourse.mybir as mybir
from concourse.bass_utils import run_bass_kernel_spmd

F32 = mybir.dt.float32
BF16 = mybir.dt.bfloat16
AF = mybir.ActivationFunctionType
ALU = mybir.AluOpType
AX = mybir.AxisListType

D = 2048
T = 2048
NCH = 16
EPS = 1e-6
W_A = 1024
W_B = 1024
W_C = 1024
OFF_A = 0
OFF_B = 4096
OFF_C = OFF_B + 4096 + 32
OFF_G = OFF_C + 4096
N_IN = OFF_G + 3 * D
NEG = -80.0
NBLK = 21

ENGS = ("pe", "act", "dve", "pool", "sp")
EPOCH = 30000


class Buf:
    __slots__ = ("name", "w", "r", "sem", "cnt", "last_dma")

    def __init__(self, name):
        self.name = name
        self.w = None
        self.r = []
        self.sem = None
        self.cnt = 0
        self.last_dma = None


class Op:
    __slots__ = ("eng", "fn", "waits", "need_inc", "tok", "dma")

    def __init__(self, eng, fn, dma=None):
        self.eng = eng
        self.fn = fn
        self.waits = []
        self.need_inc = False
        self.tok = None
        self.dma = dma


class Sync:
    def __init__(self, nc):
        self.nc = nc
        self.ops = {e: [] for e in ENGS}
        self._sem_id = 0
        self.last = {e: None for e in ENGS}
        self.dmas = []

    def new_sem(self, name):
        self._sem_id += 1
        return self.nc.alloc_semaphore(f"{name}_{self._sem_id}")

    def _deps(self, op, reads, writes):
        deps = []
        isdma = op.dma is not None
        for b in reads:
            d = b.w
            if d is not None:
                if d.dma is None and not isdma and d.eng == op.eng and op.eng == "pe":
                    pass
                else:
                    deps.append(d)
        for b in writes:
            d = b.w
            if d is not None:
                if not (d.dma is None and not isdma and d.eng == op.eng and op.eng == "pe"):
                    deps.append(d)
            for d in b.r:
                if not (d.dma is None and not isdma and d.eng == op.eng and op.eng == "pe"):
                    deps.append(d)
        seen = set()
        out = []
        for d in deps:
            if d is op or id(d) in seen:
                continue
            seen.add(id(d))
            if d.dma is None:
                d.need_inc = True
            out.append(d)
        op.waits = out

    def _update(self, op, reads, writes):
        for b in writes:
            b.w = op
            b.r = []
        for b in reads:
            if op.dma is None:
                b.r = [r for r in b.r if not (r.dma is None and r.eng == op.eng)]
            b.r.append(op)

    def op(self, eng, fn, reads=(), writes=()):
        o = Op(eng, fn)
        self._deps(o, reads, writes)
        self._update(o, reads, writes)
        self.ops[eng].append(o)
        self.last[eng] = o
        return o

    def dma(self, queue, pairs, sembuf, reads=(), writes=(), slow=False):
        if sembuf.sem is None:
            sembuf.sem = {}
            sembuf.cnt = {}
            sembuf.last_dma = {}
        if queue not in sembuf.sem:
            sembuf.sem[queue] = self.new_sem("d")
            sembuf.cnt[queue] = 0
            sembuf.last_dma[queue] = None
        o = Op(queue, pairs, dma=(sembuf, len(pairs), slow))
        self._deps(o, reads, writes)
        ld = sembuf.last_dma[queue]
        if ld is not None and ld not in o.waits:
            o.waits.append(ld)
        sembuf.cnt[queue] += 16 * len(pairs)
        o.tok = (sembuf.sem[queue], sembuf.cnt[queue])
        sembuf.last_dma[queue] = o
        self._update(o, reads, writes)
        self.ops[queue].append(o)
        self.dmas.append(o)
        return o

    def barrier(self):
        lasts = [self.last[e] for e in ENGS if self.last[e] is not None]
        dm = list(self.dmas)
        self.dmas = []
        for e in ENGS:
            o = Op(e, lambda eng: eng.nop())
            for d in lasts:
                if d.eng != e:
                    d.need_inc = True
                    o.waits.append(d)
            o.waits.extend(dm)
            self.ops[e].append(o)
            self.last[e] = o

    def emit(self):
        nc = self.nc
        for e in ENGS:
            cnt = 0
            sem = None
            for o in self.ops[e]:
                if o.dma is not None or not o.need_inc:
                    continue
                if sem is None or cnt >= EPOCH:
                    sem = self.new_sem("e" + e)
                    cnt = 0
                cnt += 1
                o.tok = (sem, cnt)
        stats = {}

        def run(e, eng):
            known = {}
            nw = 0
            for o in self.ops[e]:
                need = {}
                for d in o.waits:
                    s, v = d.tok
                    k = id(s)
                    if known.get(k, 0) >= v:
                        continue
                    if k not in need or need[k][1] < v:
                        need[k] = (s, v)
                items = list(need.items())
                attach = None
                if o.dma is None and items:
                    k, attach = items.pop()
                    known[k] = attach[1]
                for k, (s, v) in items:
                    eng.wait_ge(s, v)
                    known[k] = v
                    nw += 1
                if o.dma is not None:
                    s, _ = o.tok
                    for (out_ap, in_ap) in o.fn:
                        if o.dma[2]:
                            eng.dma_start(out=out_ap, in_=in_ap, allow_slow_non_contiguous=True).then_inc(s, 16)
                        else:
                            eng.dma_start(out=out_ap, in_=in_ap).then_inc(s, 16)
                else:
                    ins = o.fn(eng)
                    if attach is not None:
                        ins._wait_ge(attach[0], attach[1])
                    if o.need_inc:
                        ins.then_inc(o.tok[0], 1)
            stats[e] = (len(self.ops[e]), nw)

        with nc.Block() as block:
            @block.tensor
            def _(eng):
                run("pe", eng)

            @block.scalar
            def _(eng):
                run("act", eng)

            @block.vector
            def _(eng):
                run("dve", eng)

            @block.gpsimd
            def _(eng):
                run("pool", eng)

            @block.sync
            def _(eng):
                run("sp", eng)
        return stats


class Arena:
    def __init__(self, nc, name, nbytes):
        self.t = nc.alloc_sbuf_tensor(name, [128, nbytes // 2], BF16)
        self.cap = nbytes
        self.off = 0

    def reset(self, off=0):
        self.off = off

    def take(self, nelem, dt=BF16, parts=128):
        sz = nelem * (4 if dt == F32 else 2)
        sz = (sz + 63) // 64 * 64
        assert self.off + sz <= self.cap, ("arena overflow", self.off, sz, self.cap)
        a = self.t[0:parts, self.off // 2:(self.off + sz) // 2]
        self.off += sz
        if dt == F32:
            a = a.bitcast(F32)
        return a[:, 0:nelem]


class SubArena(Arena):
    def __init__(self, parent, off, size):
        self.t = parent.t
        self.off = off
        self.cap = off + size


def _att_blocks():
    blks = [(5, 5 + d) for d in (-2, -1, 0, 1, 2)]
    blks += [(0, m) for m in range(4)] + [(1, m) for m in range(4)]
    blks += [(14, m) for m in range(12, 16)] + [(15, m) for m in range(12, 16)]
    return blks


def att_block_id(n, m):
    if 2 <= n <= 13:
        return m - n + 2
    if n == 0:
        return 5 + m
    if n == 1:
        return 9 + m
    if n == 14:
        return 13 + (m - 12)
    return 17 + (m - 12)


def att_keytiles(n):
    if n <= 1:
        return [0, 1, 2, 3]
    if n >= 14:
        return [12, 13, 14, 15]
    return [n - 2, n - 1, n, n + 1, n + 2]


def _att_geometry():
    blks = _att_blocks()
    p = np.arange(128)
    krl, kc = p // 64, p % 64
    qq = np.arange(128)
    rl, c = qq // 64, qq % 64
    dr = np.zeros((NBLK, 128, 128), np.int64)
    dc = np.zeros((NBLK, 128, 128), np.int64)
    mask = np.zeros((128, NBLK * 128), np.float32)
    for b, (n, m) in enumerate(blks):
        kr = (2 * m + krl)[:, None]
        r = (2 * n + rl)[None, :]
        rs = np.clip(r - 4, 0, 24)
        cs = np.clip(c - 8, 0, 48)[None, :]
        kcc = kc[:, None]
        valid = (kr >= rs) & (kr < rs + 8) & (kcc >= cs) & (kcc < cs + 16)
        dr[b] = np.clip(kr - r + 7, 0, 14)
        dc[b] = np.clip(kcc - c[None, :] + 15, 0, 30)
        mask[:, b * 128:(b + 1) * 128] = np.where(valid, 0.0, NEG)
    return dr, dc, mask


def host_consts():
    i = np.arange(128)
    m_ = i[:, None]
    j_ = i[None, :]
    ident = np.eye(128, dtype=np.float32)
    ones = np.ones((128, 128), np.float32)
    U1 = (m_ <= j_).astype(np.float32)
    U2 = (m_ > j_).astype(np.float32)
    cf = np.concatenate([ident, ones, U1, U2, U1.T.copy(), U2.T.copy()], axis=1)
    ii, jj = m_, j_
    mk = [np.where(ii > jj, 0.0, NEG), np.where(ii >= jj, 0.0, NEG),
          np.where(ii < jj, 0.0, NEG), np.where(ii <= jj, 0.0, NEG)]
    dm = np.concatenate(mk, axis=1).astype(np.float32)
    _, _, amask = _att_geometry()
    return cf.astype(np.float32), dm, amask


def host_gm():
    i = np.arange(128)
    sb = lambda b: (i[:, None] // b == i[None, :] // b).astype(np.float32)
    U2 = (i[:, None] > i[None, :]).astype(np.float32)
    parts = [np.eye(128, dtype=np.float32), sb(8)] + [sb(2 * b) - sb(b) for b in (8, 16, 32, 64)] + [U2, U2.T.copy()]
    return np.ascontiguousarray(np.concatenate([np.concatenate([p, p], axis=1) for p in parts], axis=1))


def gather_bias(na_bias):
    dr, dc, _ = _att_geometry()
    g = na_bias[:, :, dr, dc]
    g = np.transpose(g, (0, 1, 3, 2, 4))
    return np.ascontiguousarray(g.reshape(g.shape[0], g.shape[1], 128, NBLK * 128)).astype(np.float32)


class Builder:
    def __init__(self, nseq, nlayers=2, debug=False):
        self.nseq = nseq
        self.nl = nlayers
        self.debug = debug
        self._bufc = {}
        self._phase = "init"
        self._cnt = {}
        nc = self.nc = bass.Bass("TRN2", target_bir_lowering=False)
        self.S = Sync(nc)
        di = lambda name, shape: nc.dram_tensor(name, shape, F32, kind="ExternalInput").ap()
        self.x = di("x", [nseq, T, D])
        self.norm_w = di("norm_w", [2, D])
        self.w_in = di("w_in", [2, D, N_IN])
        self.conv_a = di("conv_a", [2, 3, W_A])
        self.conv_b = di("conv_b", [2, 3, 3 * W_B])
        self.a_log = di("a_log", [2, 16])
        self.dt_bias = di("dt_bias", [2, 16])
        self.gdn_nw = di("gdn_norm_w", [2, 128])
        self.biasg = di("biasg", [2, 8, 128, NBLK * 128])
        self.w_pa = di("w_pa", [2, W_A, D])
        self.w_pb = di("w_pb", [2, W_B, D])
        self.w_pc = di("w_pc", [2, W_C, D])
        self.w_o = di("w_o", [2, D, D])
        self.fnw = di("final_norm_w", [D])
        self.cf_d = di("cf", [128, 768])
        self.dm_d = di("dm", [128, 512])
        self.am_d = di("amask", [128, NBLK * 128])
        self.gm_d = di("gm", [128, 8 * 256])
        self.y = nc.dram_tensor("y", [nseq, T, D], F32, kind="ExternalOutput").ap()
        kind = "ExternalOutput" if debug else "Internal"
        self.yscr = nc.dram_tensor("yscr", [3, 8, 128, T], BF16, kind=kind).ap()
        self.xs = [nc.dram_tensor(f"xs{i}", [T, D], F32, kind=kind).ap() for i in range(2)]
        self.b_yscr = [[self.nb(f"yscr{b}_{c}") for c in range(8)] for b in range(3)]
        self.b_xs = [[self.nb(f"xs{i}_{t}") for t in range(NCH)] for i in range(2)]
        self.b_y = self.nb("yout")

        self.P = Arena(nc, "persist", 109 * 1024)
        self.X = Arena(nc, "phase", 98 * 1024)
        P = self.P
        self.hT = P.take(16 * T).rearrange("p (k t) -> p k t", k=16)
        self.b_hT = [self.nb(f"hT{i}") for i in range(NCH)]
        self.NSLOT = 8
        self.ring = P.take(self.NSLOT * 2048).rearrange("p (s e) -> p s e", s=self.NSLOT)
        self.b_slot = [self.nb(f"slot{i}") for i in range(self.NSLOT)]
        self.slot_i = 0
        self.cf = P.take(768, F32)
        self.dmk = P.take(512, F32)
        self.cb = P.take(768)
        self.amask = P.take(NBLK * 128)
        self.b_const = self.nb("const")
        self.identf = self.cf[:, 0:128]
        self.onesf = self.cf[:, 128:256]
        self.identb = self.cb[:, 0:128]
        self.onesb = self.cb[:, 128:256]
        self.cva = P.take(8 * 3, F32).rearrange("p (c i) -> p c i", i=3)
        self.cvb = P.take(24 * 3, F32).rearrange("p (c i) -> p c i", i=3)
        self.gnwb = P.take(128, F32)
        self.dtb = P.take(1, F32)
        self.nega = P.take(1, F32)
        self.b_lp = self.nb("layerparams")
        self.cur_layer_params = None
        self.ps = [nc.alloc_psum_tensor(f"ps{i}", [128, 512], F32) for i in range(8)]
        self.b_ps = [self.nb(f"ps{i}") for i in range(8)]
        self.ps_i = 0

    def nb(self, name):
        c = self._cnt.get(name, 0)
        self._cnt[name] = c + 1
        key = (self._phase, name, c)
        if key not in self._bufc:
            self._bufc[key] = Buf(name)
        return self._bufc[key]

    def begin(self):
        import sys
        self.S.barrier()
        self.X.reset()
        self._phase = sys._getframe(1).f_code.co_name
        self._cnt = {}

    def psb(self, lo=0, hi=8):
        i = lo + (self.ps_i % (hi - lo))
        self.ps_i += 1
        return self.ps[i], self.b_ps[i]

    def load_consts(self):
        S = self.S
        S.dma("sp", [(self.cf, self.cf_d), (self.dmk, self.dm_d)], self.b_const, writes=[self.b_const])
        S.dma("pool", [(self.cb, self.cf_d), (self.amask, self.am_d)], self.b_const, writes=[self.b_const])

    def load_layer_params(self, l):
        if self.cur_layer_params == l:
            return
        self.cur_layer_params = l
        S = self.S
        nc = self.nc
        pairs = [(self.gnwb, self.gdn_nw[l].partition_broadcast(128))]
        for i in range(3):
            pairs.append((self.cva[:, :, i], self.conv_a[l, i].rearrange("(c p) -> p c", p=128)))
            pairs.append((self.cvb[:, :, i], self.conv_b[l, i].rearrange("(c p) -> p c", p=128)))
        S.dma("sp", pairs, self.b_lp, writes=[self.b_lp], slow=True)
        S.op("dve", lambda e: e.memset(self.dtb[0:32, :], 0.0), writes=[self.b_lp])
        S.op("dve", lambda e: e.memset(self.nega[0:32, :], 0.0), writes=[self.b_lp])
        S.dma("sp", [(self.dtb[16:32, :], self.dt_bias[l].rearrange("(p o) -> p o", o=1)),
                     (self.nega[16:32, :], self.a_log[l].rearrange("(p o) -> p o", o=1))],
              self.b_lp, writes=[self.b_lp], slow=True)
        S.op("act", lambda e: e.activation(out=self.nega[0:32, :], in_=self.nega[0:32, :], func=AF.Exp),
             reads=[self.b_lp], writes=[self.b_lp])
        S.op("dve", lambda e: e.tensor_scalar(out=self.nega[0:32, :], in0=self.nega[0:32, :], scalar1=-1.0,
                                              scalar2=None, op0=ALU.mult), reads=[self.b_lp], writes=[self.b_lp])

    def wslot(self, n=1):
        if self.slot_i + n > self.NSLOT:
            self.slot_i = 0
        i = self.slot_i
        self.slot_i = (self.slot_i + n) % self.NSLOT
        return i, self.b_slot[i:i + n]

    def load_w(self, dram_cols, kchunks):
        i, bufs = self.wslot(1)
        view = self.ring[:, i, 0:kchunks * 128].rearrange("p (k e) -> p k e", k=kchunks)
        self.S.dma("pool", [(view, dram_cols.rearrange("(k p) e -> p k e", p=128))], bufs[0], writes=bufs)
        return view, bufs

    def proj_tile(self, wv, wb, kchunks, src, b_src, t0, n, ps, b_ps, mrows=128, srcsel=None):
        S = self.S
        for k in range(kchunks):
            S.op("pe", lambda e, k=k: e.matmul(ps[0:mrows, 0:n], lhsT=wv[:, k, 0:mrows], rhs=src[:, k, t0:t0 + n],
                                               start=(k == 0), stop=(k == kchunks - 1)),
                 reads=list(wb) + list(b_src), writes=[b_ps])

    def hT_bufs(self, t0, n):
        return self.b_hT[t0 // 128:(t0 + n + 127) // 128]

    def phase_norm(self, src, b_src_rows, l):
        S = self.S
        X = self.X
        self.begin()
        nwb = X.take(D, F32)
        b_nwb = self.nb("nwb")
        xt = [X.take(D, F32) for _ in range(2)]
        hn = [X.take(D) for _ in range(2)]
        junk = X.take(D)
        ss = [X.take(1, F32) for _ in range(2)]
        b_xt = [self.nb("xt") for _ in range(2)]
        b_hn = [self.nb("hn") for _ in range(2)]
        b_ss = [self.nb("ss") for _ in range(2)]
        b_junk = self.nb("junk")
        S.dma("sp", [(nwb, self.norm_w[l].partition_broadcast(128))], b_nwb, writes=[b_nwb])
        for tc in range(NCH):
            i = tc % 2
            S.dma("sp", [(xt[i], src[tc * 128:(tc + 1) * 128, :])], b_xt[i], reads=[b_src_rows[tc]], writes=[b_xt[i]])
            S.op("act", lambda e, i=i: e.activation(out=junk, in_=xt[i], func=AF.Square, scale=float(D) ** -0.5,
                                                    accum_out=ss[i]), reads=[b_xt[i]], writes=[b_junk, b_ss[i]])
            S.op("act", lambda e, i=i: e.activation(out=ss[i], in_=ss[i], func=AF.Sqrt, bias=EPS),
                 reads=[b_ss[i]], writes=[b_ss[i]])
            S.op("dve", lambda e, i=i: e.reciprocal(out=ss[i], in_=ss[i]), reads=[b_ss[i]], writes=[b_ss[i]])
            S.op("dve", lambda e, i=i: e.scalar_tensor_tensor(out=hn[i], in0=xt[i], scalar=ss[i][:, 0:1], in1=nwb,
                                                              op0=ALU.mult, op1=ALU.mult),
                 reads=[b_xt[i], b_ss[i], b_nwb], writes=[b_hn[i]])
            for half in range(2):
                ps, b_ps = self.psb()
                pv = ps[:, :].bitcast(BF16)
                for k in range(8):
                    kc = half * 8 + k
                    S.op("pe", lambda e, pv=pv, k=k, kc=kc, i=i: e.transpose(
                        out=pv[:, k * 128:(k + 1) * 128], in_=hn[i][:, kc * 128:(kc + 1) * 128], identity=self.identb),
                        reads=[b_hn[i], self.b_const], writes=[b_ps])
                S.op("act", lambda e, pv=pv, half=half, tc=tc: e.copy(
                    out=self.hT[:, half * 8:(half + 1) * 8, tc * 128:(tc + 1) * 128],
                    in_=pv.rearrange("p (k t) -> p k t", k=8)), reads=[b_ps], writes=[self.b_hT[tc]])

    def phase_A(self, l):
        S = self.S
        X = self.X
        self.begin()
        u = X.take(T + 2, F32)
        bg = X.take(T, F32)
        acc = X.take(T, F32)
        ya = [X.take(T) for _ in range(2)]
        xs_ = [X.take(512, F32) for _ in range(2)]
        sg = [X.take(512, F32) for _ in range(2)]
        b_u, b_bg, b_acc = self.nb("u"), self.nb("bg"), self.nb("acc")
        b_ya = [self.nb("ya") for _ in range(2)]
        b_xs = [self.nb("xs") for _ in range(2)]
        b_sg = [self.nb("sg") for _ in range(2)]
        S.op("dve", lambda e: e.memset(u, 0.0), writes=[b_u])
        wcol = self.w_in[l]
        for c in range(8):
            ws = [self.load_w(wcol[:, OFF_A + g * W_A + c * 128: OFF_A + g * W_A + (c + 1) * 128], 16) for g in range(4)]
            for tt in range(4):
                t0 = tt * 512
                i = tt % 2
                pss = [self.psb() for _ in range(4)]
                for g in range(4):
                    self.proj_tile(ws[g][0], ws[g][1], 16, self.hT, self.hT_bufs(t0, 512), t0, 512, pss[g][0], pss[g][1])
                S.op("act", lambda e, i=i, p=pss[0][0]: e.copy(out=xs_[i], in_=p[:, :]), reads=[pss[0][1]], writes=[b_xs[i]])
                S.op("dve", lambda e, i=i, p=pss[2][0], t0=t0: e.tensor_tensor(out=u[:, 1 + t0:1 + t0 + 512], in0=p[:, :],
                                                                                in1=xs_[i], op=ALU.mult),
                     reads=[pss[2][1], b_xs[i]], writes=[b_u])
                S.op("act", lambda e, i=i, p=pss[3][0]: e.activation(out=sg[i], in_=p[:, :], func=AF.Silu),
                     reads=[pss[3][1]], writes=[b_sg[i]])
                S.op("dve", lambda e, i=i, p=pss[1][0], t0=t0: e.tensor_tensor(out=bg[:, t0:t0 + 512], in0=p[:, :],
                                                                                in1=sg[i], op=ALU.mult),
                     reads=[pss[1][1], b_sg[i]], writes=[b_bg])
            self.conv3(u, b_u, acc, b_acc, self.cva[:, c, :])
            j = c % 2
            S.op("dve", lambda e, j=j: e.tensor_tensor(out=ya[j], in0=acc, in1=bg, op=ALU.mult),
                 reads=[b_acc, b_bg], writes=[b_ya[j]])
            S.dma("sp", [(self.yscr[0, c], ya[j])], b_ya[j], reads=[b_ya[j]], writes=[self.b_yscr[0][c]])

    def conv3(self, u, b_u, acc, b_acc, w3, eng="dve"):
        S = self.S
        S.op(eng, lambda e: e.tensor_scalar(out=acc, in0=u[:, 0:T], scalar1=w3[:, 0:1], scalar2=None, op0=ALU.mult),
             reads=[b_u, self.b_lp], writes=[b_acc])
        for i in (1, 2):
            S.op(eng, lambda e, i=i: e.scalar_tensor_tensor(out=acc, in0=u[:, i:i + T], scalar=w3[:, i:i + 1], in1=acc,
                                                            op0=ALU.mult, op1=ALU.add),
                 reads=[b_u, b_acc, self.b_lp], writes=[b_acc])

    def to_token_major(self, srcT, b_src, dst, b_dst):
        S = self.S
        for half in range(2):
            ps, b_ps = self.psb()
            pv = ps[:, :].bitcast(BF16)
            for k in range(8):
                c = half * 8 + k
                S.op("pe", lambda e, pv=pv, k=k, c=c: e.transpose(out=pv[:, k * 128:(k + 1) * 128],
                                                                   in_=srcT[:, c * 128:(c + 1) * 128], identity=self.identb),
                     reads=[b_src, self.b_const], writes=[b_ps])
            S.op("act", lambda e, pv=pv, half=half: e.copy(out=dst[:, half * 8:(half + 1) * 8, :],
                                                           in_=pv.rearrange("p (k t) -> p k t", k=8)),
                 reads=[b_ps], writes=[b_dst])

    def phase_B(self, l):
        S = self.S
        X = self.X
        self.begin()
        wcol = self.w_in[l]
        cf = self.cf
        U1, U2, U1T, U2T = cf[:, 256:384], cf[:, 384:512], cf[:, 512:640], cf[:, 640:768]
        scr_off = X.off
        acc = X.take(T, F32)
        sq = X.take(T, F32)
        scr_size = X.off - scr_off
        b_acc, b_sq = self.nb("acc"), self.nb("sq")
        sf, b_sf = acc, b_acc
        bfm, gfm, b_bfm, b_gfm = acc, sq, b_acc, b_sq
        wv, wb = self.load_w(wcol[:, OFF_B + 4096: OFF_B + 4096 + 128], 16)
        for tt in range(4):
            t0 = tt * 512
            ps, b_ps = self.psb()
            self.proj_tile(wv, wb, 16, self.hT, self.hT_bufs(t0, 512), t0, 512, ps, b_ps, mrows=32)
            S.op("act", lambda e, ps=ps, t0=t0: e.activation(out=bfm[0:32, t0:t0 + 512], in_=ps[0:32, :], func=AF.Sigmoid),
                 reads=[b_ps], writes=[b_bfm])
            S.op("act", lambda e, ps=ps, t0=t0: e.activation(out=gfm[0:32, t0:t0 + 512], in_=ps[0:32, :], func=AF.Exp,
                                                             bias=self.dtb[0:32, :]),
                 reads=[b_ps, self.b_lp], writes=[b_gfm])
        S.op("act", lambda e: e.activation(out=gfm[0:32, :], in_=gfm[0:32, :], func=AF.Ln, bias=1.0),
             reads=[b_gfm], writes=[b_gfm])
        S.op("dve", lambda e: e.tensor_scalar(out=gfm[0:32, :], in0=gfm[0:32, :], scalar1=self.nega[0:32, 0:1],
                                              scalar2=None, op0=ALU.mult), reads=[b_gfm, self.b_lp], writes=[b_gfm])
        btm = X.take(NCH * 32, F32).rearrange("p (c k) -> p c k", k=32)
        gtm = X.take(NCH * 32, F32).rearrange("p (c k) -> p c k", k=32)
        b_btm, b_gtm = self.nb("btm"), self.nb("gtm")
        for (srcf, b_s, dst, b_d) in ((bfm, b_bfm, btm, b_btm), (gfm, b_gfm, gtm, b_gtm)):
            ps, b_ps = self.psb()
            for c in range(NCH):
                S.op("pe", lambda e, ps=ps, c=c, srcf=srcf: e.transpose(out=ps[:, c * 32:(c + 1) * 32],
                                                                         in_=srcf[0:32, c * 128:(c + 1) * 128],
                                                                         identity=self.identf[0:32, 0:32]),
                     reads=[b_s, self.b_const], writes=[b_ps])
            S.op("act", lambda e, ps=ps, dst=dst: e.copy(out=dst, in_=ps[:, :].rearrange("p (c k) -> p c k", k=32)),
                 reads=[b_ps], writes=[b_d])
        gcs = X.take(NCH * 48, F32).rearrange("p (c k) -> p c k", k=48)
        b_gcs = self.nb("gcs")
        for half in range(2):
            ps, b_ps = self.psb()
            for k in range(8):
                c = half * 8 + k
                for j, lt in enumerate((U1, U1T, self.onesf)):
                    S.op("pe", lambda e, ps=ps, k=k, c=c, j=j, lt=lt: e.matmul(
                        ps[:, k * 48 + j * 16:k * 48 + (j + 1) * 16], lhsT=lt, rhs=gtm[:, c, 16:32], start=True, stop=True),
                        reads=[b_gtm, self.b_const], writes=[b_ps])
            S.op("act", lambda e, ps=ps, half=half: e.copy(out=gcs[:, half * 8:(half + 1) * 8, :],
                                                           in_=ps[:, 0:384].rearrange("p (c k) -> p c k", k=48)),
                 reads=[b_ps], writes=[b_gcs])
        gcd = X.take(NCH * 16, F32).rearrange("p (c k) -> p c k", k=16)
        egc = X.take(NCH * 16, F32).rearrange("p (c k) -> p c k", k=16)
        ek = X.take(NCH * 16, F32).rearrange("p (c k) -> p c k", k=16)
        egl = X.take(NCH * 16, F32).rearrange("p (c k) -> p c k", k=16)
        nbet = X.take(NCH * 16, F32).rearrange("p (c k) -> p c k", k=16)
        begc = X.take(NCH * 16, F32).rearrange("p (c k) -> p c k", k=16)
        b_col = self.nb("cols")
        S.op("dve", lambda e: e.tensor_copy(out=gcd[:, :, 0:8], in_=gcs[:, :, 0:8]), reads=[b_gcs], writes=[b_col])
        S.op("dve", lambda e: e.tensor_copy(out=gcd[:, :, 8:16], in_=gcs[:, :, 24:32]), reads=[b_gcs], writes=[b_col])
        S.op("act", lambda e: e.activation(out=egc, in_=gcd, func=AF.Exp), reads=[b_col], writes=[b_col])
        S.op("dve", lambda e: e.tensor_tensor(out=ek, in0=gcs[:, :, 32:48], in1=gcd, op=ALU.subtract),
             reads=[b_gcs, b_col], writes=[b_col])
        S.op("act", lambda e: e.activation(out=ek, in_=ek, func=AF.Exp), reads=[b_col], writes=[b_col])
        S.op("act", lambda e: e.activation(out=egl, in_=gcs[:, :, 32:48], func=AF.Exp), reads=[b_gcs], writes=[b_col])
        S.op("dve", lambda e: e.tensor_scalar(out=nbet, in0=btm[:, :, 0:16], scalar1=-1.0, scalar2=None, op0=ALU.mult),
             reads=[b_btm], writes=[b_col])
        S.op("dve", lambda e: e.tensor_tensor(out=begc, in0=btm[:, :, 0:16], in1=egc, op=ALU.mult),
             reads=[b_btm, b_col], writes=[b_col])

        import os
        BCUT = int(os.environ.get("BCUT", "0"))
        if BCUT == 1:
            return
        u = X.take(T + 2, F32)
        b_u = self.nb("u")
        qT, kT, vT, sgate = X.take(T), X.take(T), X.take(T), X.take(T)
        b_qT, b_kT, b_vT, b_sgate = self.nb("qT"), self.nb("kT"), self.nb("vT"), self.nb("sgate")
        ktm = X.take(T).rearrange("p (c k) -> p c k", k=128)
        vtm = X.take(T).rearrange("p (c k) -> p c k", k=128)
        b_ktm, b_vtm = self.nb("ktm"), self.nb("vtm")
        oacc = X.take(T, F32).rearrange("p (c k) -> p c k", k=128)
        b_oacc = [self.nb(f"oacc{c}") for c in range(NCH)]
        onb = sq[:, 0:T // 2].bitcast(BF16).rearrange("p (c k) -> p c k", k=128)
        b_onb = b_sq
        ybT = X.take(T)
        b_ybT = self.nb("ybT")
        rn = X.take(512, F32)
        b_rn = self.nb("rn")
        ms16 = X.take(16, F32)
        b_ms16 = self.nb("ms16")
        SA = SubArena(X, scr_off, scr_size)
        tmp = []
        for si in range(4):
            R_ = X if si < 2 else SA
            d = dict(
                gU=R_.take(128, F32), dec=R_.take(256), attn=R_.take(128), AX=R_.take(384), AB8=R_.take(256),
                AB1=R_.take(256), AB2=R_.take(256), W=[R_.take(256) for _ in range(2)], XY=R_.take(256),
                rhsu=R_.take(128), rhsw=R_.take(128), kdec=R_.take(128), nwT=R_.take(128), vnew=R_.take(128),
                av=R_.take(128, F32),
            )
            d["b"] = {k: self.nb(k) for k in ("gU", "dec", "attn", "B0", "A0at", "AB8", "AB1", "AB2", "W0", "W1", "XY",
                                              "rhsu", "rhsw", "kdec", "nwT", "vnew", "av")}
            tmp.append(d)
        gmb = X.take(8 * 256)
        mkb = X.take(512)
        b_gmb = self.nb("gmb")
        S.dma("pool", [(gmb, self.gm_d), (mkb, self.dm_d)], b_gmb, writes=[b_gmb])
        ident2, sb8x2 = gmb[:, 0:256], gmb[:, 256:512]
        lvmask = [gmb[:, 512 + j * 256: 768 + j * 256] for j in range(4)]
        u2x2 = (gmb[:, 6 * 256:7 * 256], gmb[:, 7 * 256:8 * 256])
        ghb = X.take(NCH * 16).rearrange("p (c k) -> p c k", k=16)
        ghf = X.take(NCH * 16, F32).rearrange("p (c k) -> p c k", k=16)
        glf = X.take(NCH * 16, F32).rearrange("p (c k) -> p c k", k=16)
        b_ghl = self.nb("ghl")
        S.op("dve", lambda e: e.tensor_copy(out=ghb, in_=gtm[:, :, 16:32]), reads=[b_gtm], writes=[b_ghl])
        S.op("dve", lambda e: e.tensor_copy(out=ghf, in_=ghb), reads=[b_ghl], writes=[b_ghl])
        S.op("dve", lambda e: e.tensor_tensor(out=glf, in0=gtm[:, :, 16:32], in1=ghf, op=ALU.subtract),
             reads=[b_gtm, b_ghl], writes=[b_ghl])
        u1b = (self.cb[:, 256:384], self.cb[:, 512:640])
        Sf = [X.take(128, F32) for _ in range(2)]
        Sb = [X.take(128) for _ in range(2)]
        b_Sf = [self.nb("Sf") for _ in range(2)]
        b_Sb = [self.nb("Sb") for _ in range(2)]
        S.op("dve", lambda e: e.memset(u, 0.0), writes=[b_u])

        for h in range(8):
            for gi, (dstT, b_dst) in enumerate(((qT, b_qT), (kT, b_kT), (vT, b_vT))):
                col = OFF_B + gi * W_B + h * 128
                wv, wb = self.load_w(wcol[:, col:col + 128], 16)
                for tt in range(4):
                    t0 = tt * 512
                    ps, b_ps = self.psb()
                    self.proj_tile(wv, wb, 16, self.hT, self.hT_bufs(t0, 512), t0, 512, ps, b_ps)
                    S.op("act", lambda e, ps=ps, t0=t0: e.copy(out=u[:, 1 + t0:1 + t0 + 512], in_=ps[:, :]),
                         reads=[b_ps], writes=[b_u])
                self.conv3(u, b_u, acc, b_acc, self.cvb[:, gi * 8 + h, :])
                if gi == 2:
                    S.op("act", lambda e: e.activation(out=vT, in_=acc, func=AF.Silu), reads=[b_acc], writes=[b_vT])
                    continue
                S.op("act", lambda e: e.activation(out=sf, in_=acc, func=AF.Silu), reads=[b_acc], writes=[b_sf])
                S.op("act", lambda e: e.activation(out=sq, in_=sf, func=AF.Square), reads=[b_sf], writes=[b_sq])
                scale = float(128 ** -0.5) if gi == 0 else 1.0
                for tt in range(4):
                    t0 = tt * 512
                    ps, b_ps = self.psb()
                    S.op("pe", lambda e, ps=ps, t0=t0: e.matmul(ps[:, :], lhsT=self.onesf, rhs=sq[:, t0:t0 + 512],
                                                                start=True, stop=True),
                         reads=[b_sq, self.b_const], writes=[b_ps])
                    S.op("act", lambda e, ps=ps: e.activation(out=rn, in_=ps[:, :], func=AF.Sqrt, bias=EPS),
                         reads=[b_ps], writes=[b_rn])
                    S.op("dve", lambda e: e.reciprocal(out=rn, in_=rn), reads=[b_rn], writes=[b_rn])
                    S.op("dve", lambda e, t0=t0, dstT=dstT, scale=scale: e.scalar_tensor_tensor(
                        out=dstT[:, t0:t0 + 512], in0=sf[:, t0:t0 + 512], scalar=scale, in1=rn, op0=ALU.mult, op1=ALU.mult),
                        reads=[b_sf, b_rn], writes=[b_dst])
            col = OFF_B + 3 * W_B + h * 128
            wv, wb = self.load_w(wcol[:, col:col + 128], 16)
            for tt in range(4):
                t0 = tt * 512
                ps, b_ps = self.psb()
                self.proj_tile(wv, wb, 16, self.hT, self.hT_bufs(t0, 512), t0, 512, ps, b_ps)
                S.op("act", lambda e, ps=ps, t0=t0: e.activation(out=sgate[:, t0:t0 + 512], in_=ps[:, :], func=AF.Silu),
                     reads=[b_ps], writes=[b_sgate])
            self.to_token_major(kT, b_kT, ktm, b_ktm)
            self.to_token_major(vT, b_vT, vtm, b_vtm)
            if BCUT == 2:
                return

            S.barrier()
            S.op("dve", lambda e: e.memset(oacc.rearrange("p c k -> p (c k)"), 0.0), writes=b_oacc)

            def stageA(d, c, tp, h=h):
                col_ = d * 8 + h
                u1, u2 = (U1, U2) if d == 0 else (U1T, U2T)
                mk = mkb[:, d * 256:(d + 1) * 256]
                tb = tp["b"]
                AX = tp["AX"]
                attnT, A0, B0 = AX[:, 0:128], AX[:, 128:256], AX[:, 256:384]
                AB0 = AX[:, 128:384]
                bAB0 = [tb["A0at"], tb["B0"]]
                cs = slice(c * 128, (c + 1) * 128)
                gUb = tp["gU"].bitcast(BF16)
                S.op("dve", lambda e: e.tensor_scalar(out=gUb[:, 0:128], in0=u1b[d], scalar1=ghf[:, c, col_:col_ + 1], scalar2=None,
                                                      op0=ALU.mult), reads=[b_ghl, self.b_const], writes=[tb["gU"]])
                S.op("dve", lambda e: e.tensor_scalar(out=gUb[:, 128:256], in0=u1b[d], scalar1=glf[:, c, col_:col_ + 1], scalar2=None,
                                                      op0=ALU.mult), reads=[b_ghl, self.b_const], writes=[tb["gU"]])
                psD, b_psD = self.psb()
                S.op("pe", lambda e: e.matmul(psD[:, 0:256], lhsT=self.identb, rhs=mk, start=True, stop=False),
                     reads=[self.b_const, b_gmb], writes=[b_psD])
                for hf in range(2):
                    S.op("pe", lambda e, hf=hf: e.matmul(psD[:, 0:256], lhsT=gUb[:, hf * 128:(hf + 1) * 128], rhs=u2x2[d],
                                                         start=False, stop=(hf == 1)),
                         reads=[tb["gU"], b_gmb], writes=[b_psD])
                S.op("act", lambda e: e.activation(out=tp["dec"], in_=psD[:, 0:256], func=AF.Exp), reads=[b_psD], writes=[tb["dec"]])
                psK, b_psK = self.psb()
                S.op("pe", lambda e: e.matmul(psK[:, 0:128], lhsT=kT[:, cs], rhs=kT[:, cs], start=True, stop=True),
                     reads=[b_kT], writes=[b_psK])
                S.op("pe", lambda e: e.matmul(psK[:, 128:256], lhsT=qT[:, cs], rhs=kT[:, cs], start=True, stop=True),
                     reads=[b_kT, b_qT], writes=[b_psK])
                S.op("act", lambda e: e.activation(out=tp["rhsu"], in_=vtm[:, c, :], func=AF.Copy, scale=btm[:, c, col_:col_ + 1]),
                     reads=[b_vtm, b_btm], writes=[tb["rhsu"]])
                S.op("act", lambda e: e.activation(out=tp["rhsw"], in_=ktm[:, c, :], func=AF.Copy, scale=begc[:, c, col_:col_ + 1]),
                     reads=[b_ktm, b_col], writes=[tb["rhsw"]])
                S.op("act", lambda e: e.activation(out=tp["kdec"], in_=ktm[:, c, :], func=AF.Copy, scale=ek[:, c, col_:col_ + 1]),
                     reads=[b_ktm, b_col], writes=[tb["kdec"]])
                yield
                S.op("dve", lambda e: e.scalar_tensor_tensor(out=B0, in0=psK[:, 0:128], scalar=nbet[:, c, col_:col_ + 1],
                                                             in1=tp["dec"][:, 0:128], op0=ALU.mult, op1=ALU.mult),
                     reads=[b_psK, b_col, tb["dec"]], writes=[tb["B0"]])
                S.op("dve", lambda e: e.tensor_tensor(out=tp["attn"], in0=psK[:, 128:256], in1=tp["dec"][:, 128:256], op=ALU.mult),
                     reads=[b_psK, tb["dec"]], writes=[tb["attn"]])
                psT, b_psT = self.psb()
                pv = psT[:, :].bitcast(BF16)
                S.op("pe", lambda e: e.transpose(out=pv[:, 0:128], in_=tp["attn"], identity=self.identb),
                     reads=[tb["attn"], self.b_const], writes=[b_psT])
                S.op("pe", lambda e: e.transpose(out=pv[:, 128:256], in_=B0, identity=self.identb),
                     reads=[tb["B0"], self.b_const], writes=[b_psT])
                S.op("act", lambda e: e.copy(out=AX[:, 0:256], in_=pv[:, 0:256]), reads=[b_psT], writes=[tb["A0at"]])
                yield
                S.op("dve", lambda e: e.tensor_tensor(out=tp["AB8"], in0=AB0, in1=sb8x2, op=ALU.mult),
                     reads=bAB0 + [b_gmb], writes=[tb["AB8"]])
                S.op("dve", lambda e: e.tensor_tensor(out=tp["W"][0], in0=tp["AB8"], in1=ident2, op=ALU.add),
                     reads=[tb["AB8"], b_gmb], writes=[tb["W0"]])
                A8, B8 = tp["AB8"][:, 0:128], tp["AB8"][:, 128:256]
                psI, b_psI = self.psb()
                S.op("pe", lambda e: e.matmul(psI[:, 0:128], lhsT=B8, rhs=A8, start=True, stop=True), reads=[tb["AB8"]], writes=[b_psI])
                S.op("pe", lambda e: e.matmul(psI[:, 128:256], lhsT=A8, rhs=B8, start=True, stop=True), reads=[tb["AB8"]], writes=[b_psI])
                S.op("act", lambda e: e.copy(out=tp["AB1"], in_=psI[:, 0:256]), reads=[b_psI], writes=[tb["AB1"]])
                yield
                A1, B1 = tp["AB1"][:, 0:128], tp["AB1"][:, 128:256]
                psI2, b_psI2 = self.psb()
                S.op("pe", lambda e: e.matmul(psI2[:, 0:128], lhsT=B1, rhs=A1, start=True, stop=True), reads=[tb["AB1"]], writes=[b_psI2])
                S.op("pe", lambda e: e.matmul(psI2[:, 128:256], lhsT=A1, rhs=B1, start=True, stop=True), reads=[tb["AB1"]], writes=[b_psI2])
                S.op("act", lambda e: e.copy(out=tp["AB2"], in_=psI2[:, 0:256]), reads=[b_psI2], writes=[tb["AB2"]])
                W0, W1 = tp["W"][0], tp["W"][1]
                psP, b_psP = self.psb()
                S.op("pe", lambda e: e.matmul(psP[:, 0:128], lhsT=B1, rhs=W0[:, 0:128], start=True, stop=True),
                     reads=[tb["AB1"], tb["W0"]], writes=[b_psP])
                S.op("pe", lambda e: e.matmul(psP[:, 128:256], lhsT=A1, rhs=W0[:, 128:256], start=True, stop=True),
                     reads=[tb["AB1"], tb["W0"]], writes=[b_psP])
                S.op("dve", lambda e: e.tensor_tensor(out=W1, in0=psP[:, 0:256], in1=W0, op=ALU.add),
                     reads=[b_psP, tb["W0"]], writes=[tb["W1"]])
                yield
                A2, B2 = tp["AB2"][:, 0:128], tp["AB2"][:, 128:256]
                psP2, b_psP2 = self.psb()
                S.op("pe", lambda e: e.matmul(psP2[:, 0:128], lhsT=B2, rhs=W1[:, 0:128], start=True, stop=True),
                     reads=[tb["AB2"], tb["W1"]], writes=[b_psP2])
                S.op("pe", lambda e: e.matmul(psP2[:, 128:256], lhsT=A2, rhs=W1[:, 128:256], start=True, stop=True),
                     reads=[tb["AB2"], tb["W1"]], writes=[b_psP2])
                S.op("dve", lambda e: e.tensor_tensor(out=W0, in0=psP2[:, 0:256], in1=W1, op=ALU.add),
                     reads=[b_psP2, tb["W1"]], writes=[tb["W0"]])
                yield
                Wc, bWc, Wn, bWn = W0, tb["W0"], W1, tb["W1"]
                for lv in range(4):
                    TTc, Tc = Wc[:, 0:128], Wc[:, 128:256]
                    psX, b_psX = self.psb()
                    S.op("pe", lambda e, psX=psX, TTc=TTc: e.matmul(psX[:, 0:128], lhsT=B0, rhs=TTc, start=True, stop=True),
                         reads=[tb["B0"], bWc], writes=[b_psX])
                    S.op("pe", lambda e, psX=psX, Tc=Tc: e.matmul(psX[:, 128:256], lhsT=A0, rhs=Tc, start=True, stop=True),
                         reads=[tb["A0at"], bWc], writes=[b_psX])
                    S.op("dve", lambda e, psX=psX, lv=lv: e.tensor_tensor(out=tp["XY"], in0=psX[:, 0:256], in1=lvmask[lv], op=ALU.mult),
                         reads=[b_psX, b_gmb], writes=[tb["XY"]])
                    yield
                    psZ, b_psZ = self.psb()
                    S.op("pe", lambda e, psZ=psZ, Tc=Tc: e.matmul(psZ[:, 0:128], lhsT=Tc, rhs=tp["XY"][:, 0:128], start=True, stop=True),
                         reads=[bWc, tb["XY"]], writes=[b_psZ])
                    S.op("pe", lambda e, psZ=psZ, TTc=TTc: e.matmul(psZ[:, 128:256], lhsT=TTc, rhs=tp["XY"][:, 128:256], start=True, stop=True),
                         reads=[bWc, tb["XY"]], writes=[b_psZ])
                    S.op("dve", lambda e, psZ=psZ, Wn=Wn, Wc=Wc: e.tensor_tensor(out=Wn, in0=psZ[:, 0:256], in1=Wc, op=ALU.add),
                         reads=[b_psZ, bWc], writes=[bWn])
                    Wc, bWc, Wn, bWn = Wn, bWn, Wc, bWc
                    yield
                TT, bTT = Wc[:, 0:128], bWc
                psW, b_psW = self.psb()
                S.op("pe", lambda e: e.matmul(psW[:, 0:128], lhsT=tp["rhsw"], rhs=TT, start=True, stop=True),
                     reads=[tb["rhsw"], bTT], writes=[b_psW])
                S.op("act", lambda e: e.mul(out=tp["nwT"], in_=psW[:, 0:128], mul=-1.0), reads=[b_psW], writes=[tb["nwT"]])
                yield

            def tail(d, c, tp, h=h):
                col_ = d * 8 + h
                tb = tp["b"]
                cs = slice(c * 128, (c + 1) * 128)
                attnT = tp["AX"][:, 0:128]
                TT, bTT = tp["W"][0][:, 0:128], tb["W0"]
                psV, b_psV = self.psb()
                S.op("pe", lambda e: e.matmul(psV[:, 0:128], lhsT=TT, rhs=tp["rhsu"], start=True, stop=False),
                     reads=[tb["rhsu"], bTT], writes=[b_psV])
                S.op("pe", lambda e: e.matmul(psV[:, 0:128], lhsT=tp["nwT"], rhs=Sb[d], start=False, stop=True),
                     reads=[tb["nwT"], b_Sb[d]], writes=[b_psV])
                S.op("act", lambda e: e.copy(out=tp["vnew"], in_=psV[:, 0:128]), reads=[b_psV], writes=[tb["vnew"]])
                psO, b_psO = self.psb()
                S.op("pe", lambda e: e.matmul(psO[:, 0:128], lhsT=qT[:, cs], rhs=Sb[d], start=True, stop=True),
                     reads=[b_qT, b_Sb[d]], writes=[b_psO])
                yield
                S.op("pe", lambda e: e.matmul(psO[:, 128:256], lhsT=attnT, rhs=tp["vnew"], start=True, stop=True),
                     reads=[tb["A0at"], tb["vnew"]], writes=[b_psO])
                psS, b_psS = self.psb()
                S.op("pe", lambda e: e.matmul(psS[:, 0:128], lhsT=tp["kdec"], rhs=tp["vnew"], start=True, stop=True),
                     reads=[tb["kdec"], tb["vnew"]], writes=[b_psS])
                S.op("dve", lambda e: e.scalar_tensor_tensor(out=Sf[d], in0=Sf[d], scalar=egl[:, c, col_:col_ + 1], in1=psS[:, 0:128],
                                                             op0=ALU.mult, op1=ALU.add),
                     reads=[b_psS, b_col, b_Sf[d]], writes=[b_Sf[d]])
                S.op("act", lambda e: e.copy(out=Sb[d], in_=Sf[d]), reads=[b_Sf[d]], writes=[b_Sb[d]])
                S.op("dve", lambda e: e.tensor_tensor(out=tp["av"], in0=psO[:, 128:256], in1=oacc[:, c, :], op=ALU.add),
                     reads=[b_psO, b_oacc[c]], writes=[tb["av"]])
                S.op("dve", lambda e: e.scalar_tensor_tensor(out=oacc[:, c, :], in0=psO[:, 0:128], scalar=egc[:, c, col_:col_ + 1],
                                                             in1=tp["av"], op0=ALU.mult, op1=ALU.add),
                     reads=[b_psO, b_col, tb["av"]], writes=[b_oacc[c]])
                yield

            def scan(d):
                S.op("dve", lambda e: e.memset(Sf[d], 0.0), writes=[b_Sf[d]])
                S.op("dve", lambda e: e.memset(Sb[d], 0.0), writes=[b_Sb[d]])
                chunks = list(range(NCH)) if d == 0 else list(range(NCH - 1, -1, -1))
                sets = (tmp[d], tmp[2 + d])
                prev = None
                for i, c in enumerate(chunks):
                    gs = [stageA(d, c, sets[i % 2])] + ([prev] if prev is not None else [])
                    while gs:
                        for g in list(gs):
                            try:
                                next(g)
                            except StopIteration:
                                gs.remove(g)
                            yield
                    prev = tail(d, c, sets[i % 2])
                for _ in prev:
                    yield

            gens = [scan(0), scan(1)]
            while gens:
                for g in list(gens):
                    try:
                        next(g)
                    except StopIteration:
                        gens.remove(g)
            S.barrier()

            S.op("dve", lambda e: e.tensor_tensor(out=sq.rearrange("p (c k) -> p c k", k=128), in0=oacc, in1=oacc, op=ALU.mult),
                 reads=b_oacc, writes=[b_sq])
            S.op("dve", lambda e: e.tensor_reduce(out=ms16, in_=sq.rearrange("p (c k) -> p c k", k=128), axis=AX.X, op=ALU.add),
                 reads=[b_sq], writes=[b_ms16])
            S.op("act", lambda e: e.activation(out=ms16, in_=ms16, func=AF.Sqrt, scale=1.0 / 128, bias=EPS),
                 reads=[b_ms16], writes=[b_ms16])
            S.op("dve", lambda e: e.reciprocal(out=ms16, in_=ms16), reads=[b_ms16], writes=[b_ms16])
            S.op("dve", lambda e: e.tensor_tensor(out=oacc, in0=oacc, in1=ms16.unsqueeze(2).broadcast_to([128, NCH, 128]),
                                                  op=ALU.mult), reads=b_oacc + [b_ms16], writes=b_oacc)
            S.op("dve", lambda e: e.tensor_tensor(out=onb, in0=oacc, in1=self.gnwb.unsqueeze(1).broadcast_to([128, NCH, 128]),
                                                  op=ALU.mult), reads=b_oacc + [self.b_lp], writes=[b_onb])
            for half in range(2):
                ps, b_ps = self.psb()
                pv = ps[:, :].bitcast(BF16)
                for k in range(8):
                    c = half * 8 + k
                    S.op("pe", lambda e, pv=pv, k=k, c=c: e.transpose(out=pv[:, k * 128:(k + 1) * 128], in_=onb[:, c, :],
                                                                       identity=self.identb),
                         reads=[b_onb, self.b_const], writes=[b_ps])
                S.op("dve", lambda e, pv=pv, half=half: e.tensor_tensor(out=ybT[:, half * 1024:(half + 1) * 1024], in0=pv,
                                                                         in1=sgate[:, half * 1024:(half + 1) * 1024], op=ALU.mult),
                     reads=[b_ps, b_sgate], writes=[b_ybT])
            S.dma("sp", [(self.yscr[1, h], ybT)], b_ybT, reads=[b_ybT], writes=[self.b_yscr[1][h]])

    def phase_C(self, l):
        S = self.S
        X = self.X
        self.begin()
        wcol = self.w_in[l]
        qT, kT, vT, sgate = X.take(T), X.take(T), X.take(T), X.take(T)
        b_qT, b_kT, b_vT, b_sgate = self.nb("qT"), self.nb("kT"), self.nb("vT"), self.nb("sgate")
        vtm = X.take(T).rearrange("p (c k) -> p c k", k=128)
        b_vtm = self.nb("vtm")
        bias32 = X.take(NBLK * 128, F32)
        bm = X.take(NBLK * 128)
        b_bias32, b_bm = self.nb("bias32"), self.nb("bm")
        pT = [X.take(640) for _ in range(2)]
        b_pT = [self.nb("pT") for _ in range(2)]
        rden = [X.take(512, F32) for _ in range(2)]
        b_rden = [self.nb("rden") for _ in range(2)]
        on = [X.take(512, F32) for _ in range(2)]
        b_on = [self.nb("on") for _ in range(2)]
        ycT = [X.take(T) for _ in range(2)]
        b_ycT = [self.nb("ycT") for _ in range(2)]
        amf = X.take(NBLK * 128, F32)
        b_amf = self.nb("amf")
        S.op("dve", lambda e: e.tensor_copy(out=amf, in_=self.amask), reads=[self.b_const], writes=[b_amf])
        for h in range(8):
            S.dma("sp", [(bias32, self.biasg[l, h])], b_bias32, writes=[b_bias32])
            S.op("dve", lambda e: e.tensor_tensor(out=bm, in0=bias32, in1=amf, op=ALU.add),
                 reads=[b_bias32, b_amf], writes=[b_bm])
            for gi, (dstT, b_dst) in enumerate(((qT, b_qT), (kT, b_kT), (vT, b_vT), (sgate, b_sgate))):
                col = OFF_C + gi * W_C + h * 128
                wv, wb = self.load_w(wcol[:, col:col + 128], 16)
                for tt in range(4):
                    t0 = tt * 512
                    ps, b_ps = self.psb()
                    self.proj_tile(wv, wb, 16, self.hT, self.hT_bufs(t0, 512), t0, 512, ps, b_ps)
                    if gi == 0:
                        S.op("act", lambda e, ps=ps, t0=t0, dstT=dstT: e.mul(out=dstT[:, t0:t0 + 512], in_=ps[:, :],
                                                                              mul=float(128 ** -0.5)),
                             reads=[b_ps], writes=[b_dst])
                    elif gi == 3:
                        S.op("act", lambda e, ps=ps, t0=t0, dstT=dstT: e.activation(out=dstT[:, t0:t0 + 512], in_=ps[:, :],
                                                                                     func=AF.Silu), reads=[b_ps], writes=[b_dst])
                    else:
                        S.op("act", lambda e, ps=ps, t0=t0, dstT=dstT: e.copy(out=dstT[:, t0:t0 + 512], in_=ps[:, :]),
                             reads=[b_ps], writes=[b_dst])
            self.to_token_major(vT, b_vT, vtm, b_vtm)
            yc = ycT[h % 2]
            b_yc = b_ycT[h % 2]
            for n4 in range(4):
                psO, b_psO = self.psb(0, 4)
                psDn, b_psDn = self.psb(0, 4)
                i4 = n4 % 2
                for nn in range(4):
                    n = n4 * 4 + nn
                    kts = att_keytiles(n)
                    i2 = n % 2
                    psA, b_psA = self.ps[4 + 2 * i2], self.b_ps[4 + 2 * i2]
                    psB, b_psB = self.ps[5 + 2 * i2], self.b_ps[5 + 2 * i2]
                    for b, m in enumerate(kts):
                        blk = att_block_id(n, m)
                        pp, bpp = (psA, b_psA) if b < 4 else (psB, b_psB)
                        o0 = (b % 4) * 128
                        S.op("pe", lambda e, pp=pp, o0=o0, blk=blk: e.matmul(pp[:, o0:o0 + 128], lhsT=self.identb,
                                                                               rhs=bm[:, blk * 128:(blk + 1) * 128],
                                                                               start=True, stop=False),
                             reads=[b_bm, self.b_const], writes=[bpp])
                        S.op("pe", lambda e, pp=pp, o0=o0, m=m, n=n: e.matmul(pp[:, o0:o0 + 128], lhsT=kT[:, m * 128:(m + 1) * 128],
                                                                                rhs=qT[:, n * 128:(n + 1) * 128],
                                                                                start=False, stop=True),
                             reads=[b_kT, b_qT], writes=[bpp])
                    S.op("act", lambda e, psA=psA, i2=i2: e.activation(out=pT[i2][:, 0:512], in_=psA[:, :], func=AF.Exp),
                         reads=[b_psA], writes=[b_pT[i2]])
                    if len(kts) == 5:
                        S.op("act", lambda e, psB=psB, i2=i2: e.activation(out=pT[i2][:, 512:640], in_=psB[:, 0:128], func=AF.Exp),
                             reads=[b_psB], writes=[b_pT[i2]])
                    for b, m in enumerate(kts):
                        S.op("pe", lambda e, psO=psO, nn=nn, m=m, b=b, i2=i2, nk=len(kts): e.matmul(
                            psO[:, nn * 128:(nn + 1) * 128], lhsT=vtm[:, m, :], rhs=pT[i2][:, b * 128:(b + 1) * 128],
                            start=(b == 0), stop=(b == nk - 1)), reads=[b_vtm, b_pT[i2]], writes=[b_psO])
                    for b, m in enumerate(kts):
                        S.op("pe", lambda e, psDn=psDn, nn=nn, b=b, i2=i2, nk=len(kts): e.matmul(
                            psDn[:, nn * 128:(nn + 1) * 128], lhsT=self.onesb, rhs=pT[i2][:, b * 128:(b + 1) * 128],
                            start=(b == 0), stop=(b == nk - 1)), reads=[self.b_const, b_pT[i2]], writes=[b_psDn])
                t0 = n4 * 512
                S.op("dve", lambda e, psDn=psDn, i4=i4: e.reciprocal(out=rden[i4], in_=psDn[:, :]), reads=[b_psDn], writes=[b_rden[i4]])
                S.op("dve", lambda e, psO=psO, i4=i4: e.tensor_tensor(out=on[i4], in0=psO[:, :], in1=rden[i4], op=ALU.mult),
                     reads=[b_psO, b_rden[i4]], writes=[b_on[i4]])
                S.op("dve", lambda e, i4=i4, t0=t0, yc=yc: e.tensor_tensor(out=yc[:, t0:t0 + 512], in0=on[i4],
                                                                            in1=sgate[:, t0:t0 + 512], op=ALU.mult),
                     reads=[b_on[i4], b_sgate], writes=[b_yc])
            S.dma("sp", [(self.yscr[2, h], yc)], b_yc, reads=[b_yc], writes=[self.b_yscr[2][h]])

    def phase_M(self, l, src, b_src_rows, dst, b_dst_rows):
        S = self.S
        X = self.X
        self.begin()
        HALF = 1024
        yt = [X.take(8 * HALF).rearrange("p (c t) -> p c t", c=8) for _ in range(3)]
        b_yt = [self.nb(f"yt{b}") for b in range(3)]
        mT = X.take(16 * HALF).rearrange("p (k t) -> p k t", k=16)
        b_mT = [self.nb(f"mT{i}") for i in range(HALF // 128)]
        sig = [X.take(512, F32) for _ in range(2)]
        b_sig = [self.nb("sig") for _ in range(2)]
        macc = X.take(512, F32)
        b_macc = self.nb("macc")
        mtmp = X.take(512, F32)
        b_mtmp = self.nb("mtmp")
        xb = [X.take(512, F32) for _ in range(2)]
        b_xb = [self.nb("xb") for _ in range(2)]
        ob = [X.take(512, F32) for _ in range(2)]
        b_ob = [self.nb("ob") for _ in range(2)]
        wps = (self.w_pa[l], self.w_pb[l], self.w_pc[l])
        wcol = self.w_in[l]
        for th in range(2):
            tb0 = th * HALF
            for br in range(3):
                S.dma("sp", [(yt[br], self.yscr[br].rearrange("c p t -> p c t")[:, :, tb0:tb0 + HALF])], b_yt[br],
                      reads=self.b_yscr[br], writes=[b_yt[br]])
            for dmc in range(16):
                wg = [self.load_w(wcol[:, OFF_G + br * D + dmc * 128: OFF_G + br * D + (dmc + 1) * 128], 16) for br in range(3)]
                wp = [self.load_w(wps[br][:, dmc * 128:(dmc + 1) * 128], 8) for br in range(3)]
                for tt in range(2):
                    t0 = tb0 + tt * 512
                    for br in range(3):
                        psg, b_psg = self.psb()
                        psp, b_psp = self.psb()
                        self.proj_tile(wg[br][0], wg[br][1], 16, self.hT, self.hT_bufs(t0, 512), t0, 512, psg, b_psg)
                        self.proj_tile(wp[br][0], wp[br][1], 8, yt[br], [b_yt[br]], tt * 512, 512, psp, b_psp)
                        i = br % 2
                        S.op("act", lambda e, psg=psg, i=i: e.activation(out=sig[i], in_=psg[:, :], func=AF.Sigmoid),
                             reads=[b_psg], writes=[b_sig[i]])
                        mdst = mT[:, dmc, tt * 512:(tt + 1) * 512]
                        bmd = b_mT[tt * 4:(tt + 1) * 4]
                        if br == 0:
                            S.op("dve", lambda e, psp=psp, i=i: e.tensor_tensor(out=macc, in0=psp[:, :], in1=sig[i], op=ALU.mult),
                                 reads=[b_psp, b_sig[i]], writes=[b_macc])
                        else:
                            S.op("dve", lambda e, psp=psp, i=i: e.tensor_tensor(out=mtmp, in0=psp[:, :], in1=sig[i], op=ALU.mult),
                                 reads=[b_psp, b_sig[i]], writes=[b_mtmp])
                            if br == 1:
                                S.op("dve", lambda e: e.tensor_tensor(out=macc, in0=macc, in1=mtmp, op=ALU.add),
                                     reads=[b_macc, b_mtmp], writes=[b_macc])
                            else:
                                S.op("dve", lambda e, mdst=mdst: e.tensor_tensor(out=mdst, in0=macc, in1=mtmp, op=ALU.add),
                                     reads=[b_macc, b_mtmp], writes=bmd)
            for dq in range(4):
                i0, bufs = self.wslot(4)
                wov = self.ring[:, i0:i0 + 4, :].rearrange("p s e -> p (s e)").rearrange("p (k e) -> p k e", k=16)
                S.dma("pool", [(wov, self.w_o[l][:, dq * 512:(dq + 1) * 512].rearrange("(k p) e -> p k e", p=128))],
                      bufs[0], writes=bufs)
                for tcl in range(HALF // 128):
                    tc = th * (HALF // 128) + tcl
                    ps, b_ps = self.psb()
                    for k in range(16):
                        S.op("pe", lambda e, ps=ps, k=k, tcl=tcl, wov=wov: e.matmul(
                            ps[:, :], lhsT=mT[:, k, tcl * 128:(tcl + 1) * 128], rhs=wov[:, k, :], start=(k == 0), stop=(k == 15)),
                            reads=list(bufs) + [b_mT[tcl]], writes=[b_ps])
                    i = (dq * 8 + tcl) % 2
                    rows = slice(tc * 128, (tc + 1) * 128)
                    cols = slice(dq * 512, (dq + 1) * 512)
                    S.dma("sp", [(xb[i], src[rows, cols])], b_xb[i], reads=[b_src_rows[tc]], writes=[b_xb[i]])
                    S.op("dve", lambda e, ps=ps, i=i: e.tensor_tensor(out=ob[i], in0=ps[:, :], in1=xb[i], op=ALU.add),
                         reads=[b_ps, b_xb[i]], writes=[b_ob[i]])
                    S.dma("sp", [(dst[rows, cols], ob[i])], b_ob[i], reads=[b_ob[i]], writes=[b_dst_rows[tc]])

    def phase_F(self, src, b_src_rows, dst):
        S = self.S
        X = self.X
        self.begin()
        fwb = X.take(D, F32)
        b_fwb = self.nb("fwb")
        xt = [X.take(D, F32) for _ in range(2)]
        ot = [X.take(D, F32) for _ in range(2)]
        junk = X.take(D)
        ss = [X.take(1, F32) for _ in range(2)]
        b_xt = [self.nb("xt") for _ in range(2)]
        b_ot = [self.nb("ot") for _ in range(2)]
        b_ss = [self.nb("ss") for _ in range(2)]
        b_junk = self.nb("junk")
        S.dma("sp", [(fwb, self.fnw.partition_broadcast(128))], b_fwb, writes=[b_fwb])
        for tc in range(NCH):
            i = tc % 2
            rows = slice(tc * 128, (tc + 1) * 128)
            S.dma("sp", [(xt[i], src[rows, :])], b_xt[i], reads=[b_src_rows[tc]], writes=[b_xt[i]])
            S.op("act", lambda e, i=i: e.activation(out=junk, in_=xt[i], func=AF.Square, scale=float(D) ** -0.5,
                                                    accum_out=ss[i]), reads=[b_xt[i]], writes=[b_junk, b_ss[i]])
            S.op("act", lambda e, i=i: e.activation(out=ss[i], in_=ss[i], func=AF.Sqrt, bias=EPS),
                 reads=[b_ss[i]], writes=[b_ss[i]])
            S.op("dve", lambda e, i=i: e.reciprocal(out=ss[i], in_=ss[i]), reads=[b_ss[i]], writes=[b_ss[i]])
            S.op("dve", lambda e, i=i: e.scalar_tensor_tensor(out=ot[i], in0=xt[i], scalar=ss[i][:, 0:1], in1=fwb,
                                                              op0=ALU.mult, op1=ALU.mult),
                 reads=[b_xt[i], b_ss[i], b_fwb], writes=[b_ot[i]])
            S.dma("sp", [(dst[rows, :], ot[i])], b_ot[i], reads=[b_ot[i]], writes=[self.b_y])

    def build(self, phases="NABCMF"):
        S = self.S
        self.load_consts()
        for s in range(self.nseq):
            b_xin = [Buf("xin") for _ in range(NCH)]
            cur, b_cur = self.x[s], b_xin
            for l in range(self.nl):
                self.load_layer_params(l)
                nxt, b_nxt = self.xs[l % 2], self.b_xs[l % 2]
                if "N" in phases:
                    self.phase_norm(cur, b_cur, l)
                if "A" in phases:
                    self.phase_A(l)
                if "B" in phases:
                    self.phase_B(l)
                if "C" in phases:
                    self.phase_C(l)
                if "M" in phases:
                    self.phase_M(l, cur, b_cur, nxt, b_nxt)
                    cur, b_cur = nxt, b_nxt
            if "F" in phases:
                self.phase_F(cur, b_cur, self.y[s])
        S.barrier()
        self.stats = S.emit()
        return self.nc


NCORES = 8
_CACHE = {}


def kernel(x_prompt, x_sample, norm_w, w_in, conv_a, conv_b, a_log, dt_bias, gdn_norm_w, na_bias,
           w_pa, w_pb, w_pc, w_o, final_norm_w):
    f = lambda a: np.ascontiguousarray(np.asarray(a, dtype=np.float32))
    xa = np.concatenate([f(x_prompt), f(x_sample)], axis=0)
    nseq = xa.shape[0] // NCORES
    cf, dm, amask = host_consts()
    shared = {
        "norm_w": f(norm_w), "w_in": f(w_in), "conv_a": f(conv_a), "conv_b": f(conv_b),
        "a_log": f(a_log).reshape(2, 16), "dt_bias": f(dt_bias).reshape(2, 16), "gdn_norm_w": f(gdn_norm_w),
        "biasg": gather_bias(f(na_bias)), "w_pa": f(w_pa), "w_pb": f(w_pb), "w_pc": f(w_pc), "w_o": f(w_o),
        "final_norm_w": f(final_norm_w), "cf": cf, "dm": dm, "amask": amask, "gm": host_gm(),
    }
    nc = Builder(nseq).build()
    in_maps = []
    for c in range(NCORES):
        m = dict(shared)
        m["x"] = np.ascontiguousarray(xa[c * nseq:(c + 1) * nseq])
        in_maps.append(m)
    res = run_bass_kernel_spmd(nc, in_maps, core_ids=list(range(NCORES)))
    ys = np.concatenate([np.asarray(r["y"], dtype=np.float32) for r in res.results], axis=0)
    nb = x_prompt.shape[0]
    return (ys[:nb], ys[nb:])
```

```python
import numpy as np
import concourse.bass as bass
import concourse.mybir as mybir
from concourse.bass_utils import run_bass_kernel_spmd

F32 = mybir.dt.float32
BF16 = mybir.dt.bfloat16
AF = mybir.ActivationFunctionType
ALU = mybir.AluOpType
AX = mybir.AxisListType

D = 2048
T = 2048
NCH = 16
EPS = 1e-6
W_A = 1024
W_B = 1024
W_C = 1024
OFF_A = 0
OFF_B = 4096
OFF_C = OFF_B + 4096 + 32
OFF_G = OFF_C + 4096
N_IN = OFF_G + 3 * D
NEG = -80.0
NBLK = 21

ENGS = ("pe", "act", "dve", "pool", "sp")
EPOCH = 30000


class Buf:
    __slots__ = ("name", "w", "r", "sem", "cnt", "last_dma")

    def __init__(self, name):
        self.name = name
        self.w = None
        self.r = []
        self.sem = None
        self.cnt = 0
        self.last_dma = None


class Op:
    __slots__ = ("eng", "fn", "waits", "need_inc", "tok", "dma")

    def __init__(self, eng, fn, dma=None):
        self.eng = eng
        self.fn = fn
        self.waits = []
        self.need_inc = False
        self.tok = None
        self.dma = dma


class Sync:
    def __init__(self, nc):
        self.nc = nc
        self.ops = {e: [] for e in ENGS}
        self._sem_id = 0
        self.last = {e: None for e in ENGS}
        self.dmas = []

    def new_sem(self, name):
        self._sem_id += 1
        return self.nc.alloc_semaphore(f"{name}_{self._sem_id}")

    def _deps(self, op, reads, writes):
        deps = []
        isdma = op.dma is not None
        for b in reads:
            d = b.w
            if d is not None:
                if d.dma is None and not isdma and d.eng == op.eng and op.eng == "pe":
                    pass
                else:
                    deps.append(d)
        for b in writes:
            d = b.w
            if d is not None:
                if not (d.dma is None and not isdma and d.eng == op.eng and op.eng == "pe"):
                    deps.append(d)
            for d in b.r:
                if not (d.dma is None and not isdma and d.eng == op.eng and op.eng == "pe"):
                    deps.append(d)
        seen = set()
        out = []
        for d in deps:
            if d is op or id(d) in seen:
                continue
            seen.add(id(d))
            if d.dma is None:
                d.need_inc = True
            out.append(d)
        op.waits = out

    def _update(self, op, reads, writes):
        for b in writes:
            b.w = op
            b.r = []
        for b in reads:
            if op.dma is None:
                b.r = [r for r in b.r if not (r.dma is None and r.eng == op.eng)]
            b.r.append(op)

    def op(self, eng, fn, reads=(), writes=()):
        o = Op(eng, fn)
        self._deps(o, reads, writes)
        self._update(o, reads, writes)
        self.ops[eng].append(o)
        self.last[eng] = o
        return o

    def dma(self, queue, pairs, sembuf, reads=(), writes=(), slow=False):
        if sembuf.sem is None:
            sembuf.sem = {}
            sembuf.cnt = {}
            sembuf.last_dma = {}
        if queue not in sembuf.sem:
            sembuf.sem[queue] = self.new_sem("d")
            sembuf.cnt[queue] = 0
            sembuf.last_dma[queue] = None
        o = Op(queue, pairs, dma=(sembuf, len(pairs), slow))
        self._deps(o, reads, writes)
        ld = sembuf.last_dma[queue]
        if ld is not None and ld not in o.waits:
            o.waits.append(ld)
        sembuf.cnt[queue] += 16 * len(pairs)
        o.tok = (sembuf.sem[queue], sembuf.cnt[queue])
        sembuf.last_dma[queue] = o
        self._update(o, reads, writes)
        self.ops[queue].append(o)
        self.dmas.append(o)
        return o

    def barrier(self):
        lasts = [self.last[e] for e in ENGS if self.last[e] is not None]
        dm = list(self.dmas)
        self.dmas = []
        for e in ENGS:
            o = Op(e, lambda eng: eng.nop())
            for d in lasts:
                if d.eng != e:
                    d.need_inc = True
                    o.waits.append(d)
            o.waits.extend(dm)
            self.ops[e].append(o)
            self.last[e] = o

    def emit(self):
        nc = self.nc
        for e in ENGS:
            cnt = 0
            sem = None
            for o in self.ops[e]:
                if o.dma is not None or not o.need_inc:
                    continue
                if sem is None or cnt >= EPOCH:
                    sem = self.new_sem("e" + e)
                    cnt = 0
                cnt += 1
                o.tok = (sem, cnt)
        stats = {}

        def run(e, eng):
            known = {}
            nw = 0
            for o in self.ops[e]:
                need = {}
                for d in o.waits:
                    s, v = d.tok
                    k = id(s)
                    if known.get(k, 0) >= v:
                        continue
                    if k not in need or need[k][1] < v:
                        need[k] = (s, v)
                items = list(need.items())
                attach = None
                if o.dma is None and items:
                    k, attach = items.pop()
                    known[k] = attach[1]
                for k, (s, v) in items:
                    eng.wait_ge(s, v)
                    known[k] = v
                    nw += 1
                if o.dma is not None:
                    s, _ = o.tok
                    for (out_ap, in_ap) in o.fn:
                        if o.dma[2]:
                            eng.dma_start(out=out_ap, in_=in_ap, allow_slow_non_contiguous=True).then_inc(s, 16)
                        else:
                            eng.dma_start(out=out_ap, in_=in_ap).then_inc(s, 16)
                else:
                    ins = o.fn(eng)
                    if attach is not None:
                        ins._wait_ge(attach[0], attach[1])
                    if o.need_inc:
                        ins.then_inc(o.tok[0], 1)
            stats[e] = (len(self.ops[e]), nw)

        with nc.Block() as block:
            @block.tensor
            def _(eng):
                run("pe", eng)

            @block.scalar
            def _(eng):
                run("act", eng)

            @block.vector
            def _(eng):
                run("dve", eng)

            @block.gpsimd
            def _(eng):
                run("pool", eng)

            @block.sync
            def _(eng):
                run("sp", eng)
        return stats


class Arena:
    def __init__(self, nc, name, nbytes):
        self.t = nc.alloc_sbuf_tensor(name, [128, nbytes // 2], BF16)
        self.cap = nbytes
        self.off = 0

    def reset(self, off=0):
        self.off = off

    def take(self, nelem, dt=BF16, parts=128):
        sz = nelem * (4 if dt == F32 else 2)
        sz = (sz + 63) // 64 * 64
        assert self.off + sz <= self.cap, ("arena overflow", self.off, sz, self.cap)
        a = self.t[0:parts, self.off // 2:(self.off + sz) // 2]
        self.off += sz
        if dt == F32:
            a = a.bitcast(F32)
        return a[:, 0:nelem]


class SubArena(Arena):
    def __init__(self, parent, off, size):
        self.t = parent.t
        self.off = off
        self.cap = off + size


def _att_blocks():
    blks = [(5, 5 + d) for d in (-2, -1, 0, 1, 2)]
    blks += [(0, m) for m in range(4)] + [(1, m) for m in range(4)]
    blks += [(14, m) for m in range(12, 16)] + [(15, m) for m in range(12, 16)]
    return blks


def att_block_id(n, m):
    if 2 <= n <= 13:
        return m - n + 2
    if n == 0:
        return 5 + m
    if n == 1:
        return 9 + m
    if n == 14:
        return 13 + (m - 12)
    return 17 + (m - 12)


def att_keytiles(n):
    if n <= 1:
        return [0, 1, 2, 3]
    if n >= 14:
        return [12, 13, 14, 15]
    return [n - 2, n - 1, n, n + 1, n + 2]


def _att_geometry():
    blks = _att_blocks()
    p = np.arange(128)
    krl, kc = p // 64, p % 64
    qq = np.arange(128)
    rl, c = qq // 64, qq % 64
    dr = np.zeros((NBLK, 128, 128), np.int64)
    dc = np.zeros((NBLK, 128, 128), np.int64)
    mask = np.zeros((128, NBLK * 128), np.float32)
    for b, (n, m) in enumerate(blks):
        kr = (2 * m + krl)[:, None]
        r = (2 * n + rl)[None, :]
        rs = np.clip(r - 4, 0, 24)
        cs = np.clip(c - 8, 0, 48)[None, :]
        kcc = kc[:, None]
        valid = (kr >= rs) & (kr < rs + 8) & (kcc >= cs) & (kcc < cs + 16)
        dr[b] = np.clip(kr - r + 7, 0, 14)
        dc[b] = np.clip(kcc - c[None, :] + 15, 0, 30)
        mask[:, b * 128:(b + 1) * 128] = np.where(valid, 0.0, NEG)
    return dr, dc, mask


def host_consts():
    i = np.arange(128)
    m_ = i[:, None]
    j_ = i[None, :]
    ident = np.eye(128, dtype=np.float32)
    ones = np.ones((128, 128), np.float32)
    U1 = (m_ <= j_).astype(np.float32)
    U2 = (m_ > j_).astype(np.float32)
    cf = np.concatenate([ident, ones, U1, U2, U1.T.copy(), U2.T.copy()], axis=1)
    ii, jj = m_, j_
    mk = [np.where(ii > jj, 0.0, NEG), np.where(ii >= jj, 0.0, NEG),
          np.where(ii < jj, 0.0, NEG), np.where(ii <= jj, 0.0, NEG)]
    dm = np.concatenate(mk, axis=1).astype(np.float32)
    _, _, amask = _att_geometry()
    return cf.astype(np.float32), dm, amask


def host_gm():
    i = np.arange(128)
    sb = lambda b: (i[:, None] // b == i[None, :] // b).astype(np.float32)
    U2 = (i[:, None] > i[None, :]).astype(np.float32)
    parts = [np.eye(128, dtype=np.float32), sb(8)] + [sb(2 * b) - sb(b) for b in (8, 16, 32, 64)] + [U2, U2.T.copy()]
    return np.ascontiguousarray(np.concatenate([np.concatenate([p, p], axis=1) for p in parts], axis=1))


def gather_bias(na_bias):
    dr, dc, _ = _att_geometry()
    g = na_bias[:, :, dr, dc]
    g = np.transpose(g, (0, 1, 3, 2, 4))
    return np.ascontiguousarray(g.reshape(g.shape[0], g.shape[1], 128, NBLK * 128)).astype(np.float32)


class Builder:
    def __init__(self, nseq, nlayers=2, debug=False):
        self.nseq = nseq
        self.nl = nlayers
        self.debug = debug
        self._bufc = {}
        self._phase = "init"
        self._cnt = {}
        nc = self.nc = bass.Bass("TRN2", target_bir_lowering=False)
        self.S = Sync(nc)
        di = lambda name, shape: nc.dram_tensor(name, shape, F32, kind="ExternalInput").ap()
        self.x = di("x", [nseq, T, D])
        self.norm_w = di("norm_w", [2, D])
        self.w_in = di("w_in", [2, D, N_IN])
        self.conv_a = di("conv_a", [2, 3, W_A])
        self.conv_b = di("conv_b", [2, 3, 3 * W_B])
        self.a_log = di("a_log", [2, 16])
        self.dt_bias = di("dt_bias", [2, 16])
        self.gdn_nw = di("gdn_norm_w", [2, 128])
        self.biasg = di("biasg", [2, 8, 128, NBLK * 128])
        self.w_pa = di("w_pa", [2, W_A, D])
        self.w_pb = di("w_pb", [2, W_B, D])
        self.w_pc = di("w_pc", [2, W_C, D])
        self.w_o = di("w_o", [2, D, D])
        self.fnw = di("final_norm_w", [D])
        self.cf_d = di("cf", [128, 768])
        self.dm_d = di("dm", [128, 512])
        self.am_d = di("amask", [128, NBLK * 128])
        self.gm_d = di("gm", [128, 8 * 256])
        self.y = nc.dram_tensor("y", [nseq, T, D], F32, kind="ExternalOutput").ap()
        kind = "ExternalOutput" if debug else "Internal"
        self.yscr = nc.dram_tensor("yscr", [3, 8, 128, T], BF16, kind=kind).ap()
        self.xs = [nc.dram_tensor(f"xs{i}", [T, D], F32, kind=kind).ap() for i in range(2)]
        self.b_yscr = [[self.nb(f"yscr{b}_{c}") for c in range(8)] for b in range(3)]
        self.b_xs = [[self.nb(f"xs{i}_{t}") for t in range(NCH)] for i in range(2)]
        self.b_y = self.nb("yout")

        self.P = Arena(nc, "persist", 109 * 1024)
        self.X = Arena(nc, "phase", 98 * 1024)
        P = self.P
        self.hT = P.take(16 * T).rearrange("p (k t) -> p k t", k=16)
        self.b_hT = [self.nb(f"hT{i}") for i in range(NCH)]
        self.NSLOT = 8
        self.ring = P.take(self.NSLOT * 2048).rearrange("p (s e) -> p s e", s=self.NSLOT)
        self.b_slot = [self.nb(f"slot{i}") for i in range(self.NSLOT)]
        self.slot_i = 0
        self.cf = P.take(768, F32)
        self.dmk = P.take(512, F32)
        self.cb = P.take(768)
        self.amask = P.take(NBLK * 128)
        self.b_const = self.nb("const")
        self.identf = self.cf[:, 0:128]
        self.onesf = self.cf[:, 128:256]
        self.identb = self.cb[:, 0:128]
        self.onesb = self.cb[:, 128:256]
        self.cva = P.take(8 * 3, F32).rearrange("p (c i) -> p c i", i=3)
        self.cvb = P.take(24 * 3, F32).rearrange("p (c i) -> p c i", i=3)
        self.gnwb = P.take(128, F32)
        self.dtb = P.take(1, F32)
        self.nega = P.take(1, F32)
        self.b_lp = self.nb("layerparams")
        self.cur_layer_params = None
        self.ps = [nc.alloc_psum_tensor(f"ps{i}", [128, 512], F32) for i in range(8)]
        self.b_ps = [self.nb(f"ps{i}") for i in range(8)]
        self.ps_i = 0

    def nb(self, name):
        c = self._cnt.get(name, 0)
        self._cnt[name] = c + 1
        key = (self._phase, name, c)
        if key not in self._bufc:
            self._bufc[key] = Buf(name)
        return self._bufc[key]

    def begin(self):
        import sys
        self.S.barrier()
        self.X.reset()
        self._phase = sys._getframe(1).f_code.co_name
        self._cnt = {}

    def psb(self, lo=0, hi=8):
        i = lo + (self.ps_i % (hi - lo))
        self.ps_i += 1
        return self.ps[i], self.b_ps[i]

    def load_consts(self):
        S = self.S
        S.dma("sp", [(self.cf, self.cf_d), (self.dmk, self.dm_d)], self.b_const, writes=[self.b_const])
        S.dma("pool", [(self.cb, self.cf_d), (self.amask, self.am_d)], self.b_const, writes=[self.b_const])

    def load_layer_params(self, l):
        if self.cur_layer_params == l:
            return
        self.cur_layer_params = l
        S = self.S
        nc = self.nc
        pairs = [(self.gnwb, self.gdn_nw[l].partition_broadcast(128))]
        for i in range(3):
            pairs.append((self.cva[:, :, i], self.conv_a[l, i].rearrange("(c p) -> p c", p=128)))
            pairs.append((self.cvb[:, :, i], self.conv_b[l, i].rearrange("(c p) -> p c", p=128)))
        S.dma("sp", pairs, self.b_lp, writes=[self.b_lp], slow=True)
        S.op("dve", lambda e: e.memset(self.dtb[0:32, :], 0.0), writes=[self.b_lp])
        S.op("dve", lambda e: e.memset(self.nega[0:32, :], 0.0), writes=[self.b_lp])
        S.dma("sp", [(self.dtb[16:32, :], self.dt_bias[l].rearrange("(p o) -> p o", o=1)),
                     (self.nega[16:32, :], self.a_log[l].rearrange("(p o) -> p o", o=1))],
              self.b_lp, writes=[self.b_lp], slow=True)
        S.op("act", lambda e: e.activation(out=self.nega[0:32, :], in_=self.nega[0:32, :], func=AF.Exp),
             reads=[self.b_lp], writes=[self.b_lp])
        S.op("dve", lambda e: e.tensor_scalar(out=self.nega[0:32, :], in0=self.nega[0:32, :], scalar1=-1.0,
                                              scalar2=None, op0=ALU.mult), reads=[self.b_lp], writes=[self.b_lp])

    def wslot(self, n=1):
        if self.slot_i + n > self.NSLOT:
            self.slot_i = 0
        i = self.slot_i
        self.slot_i = (self.slot_i + n) % self.NSLOT
        return i, self.b_slot[i:i + n]

    def load_w(self, dram_cols, kchunks):
        i, bufs = self.wslot(1)
        view = self.ring[:, i, 0:kchunks * 128].rearrange("p (k e) -> p k e", k=kchunks)
        self.S.dma("pool", [(view, dram_cols.rearrange("(k p) e -> p k e", p=128))], bufs[0], writes=bufs)
        return view, bufs

    def proj_tile(self, wv, wb, kchunks, src, b_src, t0, n, ps, b_ps, mrows=128, srcsel=None):
        S = self.S
        for k in range(kchunks):
            S.op("pe", lambda e, k=k: e.matmul(ps[0:mrows, 0:n], lhsT=wv[:, k, 0:mrows], rhs=src[:, k, t0:t0 + n],
                                               start=(k == 0), stop=(k == kchunks - 1)),
                 reads=list(wb) + list(b_src), writes=[b_ps])

    def hT_bufs(self, t0, n):
        return self.b_hT[t0 // 128:(t0 + n + 127) // 128]

    def phase_norm(self, src, b_src_rows, l):
        S = self.S
        X = self.X
        self.begin()
        nwb = X.take(D, F32)
        b_nwb = self.nb("nwb")
        xt = [X.take(D, F32) for _ in range(2)]
        hn = [X.take(D) for _ in range(2)]
        junk = X.take(D)
        ss = [X.take(1, F32) for _ in range(2)]
        b_xt = [self.nb("xt") for _ in range(2)]
        b_hn = [self.nb("hn") for _ in range(2)]
        b_ss = [self.nb("ss") for _ in range(2)]
        b_junk = self.nb("junk")
        S.dma("sp", [(nwb, self.norm_w[l].partition_broadcast(128))], b_nwb, writes=[b_nwb])
        for tc in range(NCH):
            i = tc % 2
            S.dma("sp", [(xt[i], src[tc * 128:(tc + 1) * 128, :])], b_xt[i], reads=[b_src_rows[tc]], writes=[b_xt[i]])
            S.op("act", lambda e, i=i: e.activation(out=junk, in_=xt[i], func=AF.Square, scale=float(D) ** -0.5,
                                                    accum_out=ss[i]), reads=[b_xt[i]], writes=[b_junk, b_ss[i]])
            S.op("act", lambda e, i=i: e.activation(out=ss[i], in_=ss[i], func=AF.Sqrt, bias=EPS),
                 reads=[b_ss[i]], writes=[b_ss[i]])
            S.op("dve", lambda e, i=i: e.reciprocal(out=ss[i], in_=ss[i]), reads=[b_ss[i]], writes=[b_ss[i]])
            S.op("dve", lambda e, i=i: e.scalar_tensor_tensor(out=hn[i], in0=xt[i], scalar=ss[i][:, 0:1], in1=nwb,
                                                              op0=ALU.mult, op1=ALU.mult),
                 reads=[b_xt[i], b_ss[i], b_nwb], writes=[b_hn[i]])
            for half in range(2):
                ps, b_ps = self.psb()
                pv = ps[:, :].bitcast(BF16)
                for k in range(8):
                    kc = half * 8 + k
                    S.op("pe", lambda e, pv=pv, k=k, kc=kc, i=i: e.transpose(
                        out=pv[:, k * 128:(k + 1) * 128], in_=hn[i][:, kc * 128:(kc + 1) * 128], identity=self.identb),
                        reads=[b_hn[i], self.b_const], writes=[b_ps])
                S.op("act", lambda e, pv=pv, half=half, tc=tc: e.copy(
                    out=self.hT[:, half * 8:(half + 1) * 8, tc * 128:(tc + 1) * 128],
                    in_=pv.rearrange("p (k t) -> p k t", k=8)), reads=[b_ps], writes=[self.b_hT[tc]])

    def phase_A(self, l):
        S = self.S
        X = self.X
        self.begin()
        u = X.take(T + 2, F32)
        bg = X.take(T, F32)
        acc = X.take(T, F32)
        ya = [X.take(T) for _ in range(2)]
        xs_ = [X.take(512, F32) for _ in range(2)]
        sg = [X.take(512, F32) for _ in range(2)]
        b_u, b_bg, b_acc = self.nb("u"), self.nb("bg"), self.nb("acc")
        b_ya = [self.nb("ya") for _ in range(2)]
        b_xs = [self.nb("xs") for _ in range(2)]
        b_sg = [self.nb("sg") for _ in range(2)]
        S.op("dve", lambda e: e.memset(u, 0.0), writes=[b_u])
        wcol = self.w_in[l]
        for c in range(8):
            ws = [self.load_w(wcol[:, OFF_A + g * W_A + c * 128: OFF_A + g * W_A + (c + 1) * 128], 16) for g in range(4)]
            for tt in range(4):
                t0 = tt * 512
                i = tt % 2
                pss = [self.psb() for _ in range(4)]
                for g in range(4):
                    self.proj_tile(ws[g][0], ws[g][1], 16, self.hT, self.hT_bufs(t0, 512), t0, 512, pss[g][0], pss[g][1])
                S.op("act", lambda e, i=i, p=pss[0][0]: e.copy(out=xs_[i], in_=p[:, :]), reads=[pss[0][1]], writes=[b_xs[i]])
                S.op("dve", lambda e, i=i, p=pss[2][0], t0=t0: e.tensor_tensor(out=u[:, 1 + t0:1 + t0 + 512], in0=p[:, :],
                                                                                in1=xs_[i], op=ALU.mult),
                     reads=[pss[2][1], b_xs[i]], writes=[b_u])
                S.op("act", lambda e, i=i, p=pss[3][0]: e.activation(out=sg[i], in_=p[:, :], func=AF.Silu),
                     reads=[pss[3][1]], writes=[b_sg[i]])
                S.op("dve", lambda e, i=i, p=pss[1][0], t0=t0: e.tensor_tensor(out=bg[:, t0:t0 + 512], in0=p[:, :],
                                                                                in1=sg[i], op=ALU.mult),
                     reads=[pss[1][1], b_sg[i]], writes=[b_bg])
            self.conv3(u, b_u, acc, b_acc, self.cva[:, c, :])
            j = c % 2
            S.op("dve", lambda e, j=j: e.tensor_tensor(out=ya[j], in0=acc, in1=bg, op=ALU.mult),
                 reads=[b_acc, b_bg], writes=[b_ya[j]])
            S.dma("sp", [(self.yscr[0, c], ya[j])], b_ya[j], reads=[b_ya[j]], writes=[self.b_yscr[0][c]])

    def conv3(self, u, b_u, acc, b_acc, w3, eng="dve"):
        S = self.S
        S.op(eng, lambda e: e.tensor_scalar(out=acc, in0=u[:, 0:T], scalar1=w3[:, 0:1], scalar2=None, op0=ALU.mult),
             reads=[b_u, self.b_lp], writes=[b_acc])
        for i in (1, 2):
            S.op(eng, lambda e, i=i: e.scalar_tensor_tensor(out=acc, in0=u[:, i:i + T], scalar=w3[:, i:i + 1], in1=acc,
                                                            op0=ALU.mult, op1=ALU.add),
                 reads=[b_u, b_acc, self.b_lp], writes=[b_acc])

    def to_token_major(self, srcT, b_src, dst, b_dst):
        S = self.S
        for half in range(2):
            ps, b_ps = self.psb()
            pv = ps[:, :].bitcast(BF16)
            for k in range(8):
                c = half * 8 + k
                S.op("pe", lambda e, pv=pv, k=k, c=c: e.transpose(out=pv[:, k * 128:(k + 1) * 128],
                                                                   in_=srcT[:, c * 128:(c + 1) * 128], identity=self.identb),
                     reads=[b_src, self.b_const], writes=[b_ps])
            S.op("act", lambda e, pv=pv, half=half: e.copy(out=dst[:, half * 8:(half + 1) * 8, :],
                                                           in_=pv.rearrange("p (k t) -> p k t", k=8)),
                 reads=[b_ps], writes=[b_dst])

    def phase_B(self, l):
        S = self.S
        X = self.X
        self.begin()
        wcol = self.w_in[l]
        cf = self.cf
        U1, U2, U1T, U2T = cf[:, 256:384], cf[:, 384:512], cf[:, 512:640], cf[:, 640:768]
        scr_off = X.off
        acc = X.take(T, F32)
        sq = X.take(T, F32)
        scr_size = X.off - scr_off
        b_acc, b_sq = self.nb("acc"), self.nb("sq")
        sf, b_sf = acc, b_acc
        bfm, gfm, b_bfm, b_gfm = acc, sq, b_acc, b_sq
        wv, wb = self.load_w(wcol[:, OFF_B + 4096: OFF_B + 4096 + 128], 16)
        for tt in range(4):
            t0 = tt * 512
            ps, b_ps = self.psb()
            self.proj_tile(wv, wb, 16, self.hT, self.hT_bufs(t0, 512), t0, 512, ps, b_ps, mrows=32)
            S.op("act", lambda e, ps=ps, t0=t0: e.activation(out=bfm[0:32, t0:t0 + 512], in_=ps[0:32, :], func=AF.Sigmoid),
                 reads=[b_ps], writes=[b_bfm])
            S.op("act", lambda e, ps=ps, t0=t0: e.activation(out=gfm[0:32, t0:t0 + 512], in_=ps[0:32, :], func=AF.Exp,
                                                             bias=self.dtb[0:32, :]),
                 reads=[b_ps, self.b_lp], writes=[b_gfm])
        S.op("act", lambda e: e.activation(out=gfm[0:32, :], in_=gfm[0:32, :], func=AF.Ln, bias=1.0),
             reads=[b_gfm], writes=[b_gfm])
        S.op("dve", lambda e: e.tensor_scalar(out=gfm[0:32, :], in0=gfm[0:32, :], scalar1=self.nega[0:32, 0:1],
                                              scalar2=None, op0=ALU.mult), reads=[b_gfm, self.b_lp], writes=[b_gfm])
        btm = X.take(NCH * 32, F32).rearrange("p (c k) -> p c k", k=32)
        gtm = X.take(NCH * 32, F32).rearrange("p (c k) -> p c k", k=32)
        b_btm, b_gtm = self.nb("btm"), self.nb("gtm")
        for (srcf, b_s, dst, b_d) in ((bfm, b_bfm, btm, b_btm), (gfm, b_gfm, gtm, b_gtm)):
            ps, b_ps = self.psb()
            for c in range(NCH):
                S.op("pe", lambda e, ps=ps, c=c, srcf=srcf: e.transpose(out=ps[:, c * 32:(c + 1) * 32],
                                                                         in_=srcf[0:32, c * 128:(c + 1) * 128],
                                                                         identity=self.identf[0:32, 0:32]),
                     reads=[b_s, self.b_const], writes=[b_ps])
            S.op("act", lambda e, ps=ps, dst=dst: e.copy(out=dst, in_=ps[:, :].rearrange("p (c k) -> p c k", k=32)),
                 reads=[b_ps], writes=[b_d])
        gcs = X.take(NCH * 48, F32).rearrange("p (c k) -> p c k", k=48)
        b_gcs = self.nb("gcs")
        for half in range(2):
            ps, b_ps = self.psb()
            for k in range(8):
                c = half * 8 + k
                for j, lt in enumerate((U1, U1T, self.onesf)):
                    S.op("pe", lambda e, ps=ps, k=k, c=c, j=j, lt=lt: e.matmul(
                        ps[:, k * 48 + j * 16:k * 48 + (j + 1) * 16], lhsT=lt, rhs=gtm[:, c, 16:32], start=True, stop=True),
                        reads=[b_gtm, self.b_const], writes=[b_ps])
            S.op("act", lambda e, ps=ps, half=half: e.copy(out=gcs[:, half * 8:(half + 1) * 8, :],
                                                           in_=ps[:, 0:384].rearrange("p (c k) -> p c k", k=48)),
                 reads=[b_ps], writes=[b_gcs])
        gcd = X.take(NCH * 16, F32).rearrange("p (c k) -> p c k", k=16)
        egc = X.take(NCH * 16, F32).rearrange("p (c k) -> p c k", k=16)
        ek = X.take(NCH * 16, F32).rearrange("p (c k) -> p c k", k=16)
        egl = X.take(NCH * 16, F32).rearrange("p (c k) -> p c k", k=16)
        nbet = X.take(NCH * 16, F32).rearrange("p (c k) -> p c k", k=16)
        begc = X.take(NCH * 16, F32).rearrange("p (c k) -> p c k", k=16)
        b_col = self.nb("cols")
        S.op("dve", lambda e: e.tensor_copy(out=gcd[:, :, 0:8], in_=gcs[:, :, 0:8]), reads=[b_gcs], writes=[b_col])
        S.op("dve", lambda e: e.tensor_copy(out=gcd[:, :, 8:16], in_=gcs[:, :, 24:32]), reads=[b_gcs], writes=[b_col])
        S.op("act", lambda e: e.activation(out=egc, in_=gcd, func=AF.Exp), reads=[b_col], writes=[b_col])
        S.op("dve", lambda e: e.tensor_tensor(out=ek, in0=gcs[:, :, 32:48], in1=gcd, op=ALU.subtract),
             reads=[b_gcs, b_col], writes=[b_col])
        S.op("act", lambda e: e.activation(out=ek, in_=ek, func=AF.Exp), reads=[b_col], writes=[b_col])
        S.op("act", lambda e: e.activation(out=egl, in_=gcs[:, :, 32:48], func=AF.Exp), reads=[b_gcs], writes=[b_col])
        S.op("dve", lambda e: e.tensor_scalar(out=nbet, in0=btm[:, :, 0:16], scalar1=-1.0, scalar2=None, op0=ALU.mult),
             reads=[b_btm], writes=[b_col])
        S.op("dve", lambda e: e.tensor_tensor(out=begc, in0=btm[:, :, 0:16], in1=egc, op=ALU.mult),
             reads=[b_btm, b_col], writes=[b_col])

        import os
        BCUT = int(os.environ.get("BCUT", "0"))
        if BCUT == 1:
            return
        u = X.take(T + 2, F32)
        b_u = self.nb("u")
        qT, kT, vT, sgate = X.take(T), X.take(T), X.take(T), X.take(T)
        b_qT, b_kT, b_vT, b_sgate = self.nb("qT"), self.nb("kT"), self.nb("vT"), self.nb("sgate")
        ktm = X.take(T).rearrange("p (c k) -> p c k", k=128)
        vtm = X.take(T).rearrange("p (c k) -> p c k", k=128)
        b_ktm, b_vtm = self.nb("ktm"), self.nb("vtm")
        oacc = X.take(T, F32).rearrange("p (c k) -> p c k", k=128)
        b_oacc = [self.nb(f"oacc{c}") for c in range(NCH)]
        onb = sq[:, 0:T // 2].bitcast(BF16).rearrange("p (c k) -> p c k", k=128)
        b_onb = b_sq
        ybT = X.take(T)
        b_ybT = self.nb("ybT")
        rn = X.take(512, F32)
        b_rn = self.nb("rn")
        ms16 = X.take(16, F32)
        b_ms16 = self.nb("ms16")
        SA = SubArena(X, scr_off, scr_size)
        tmp = []
        for si in range(4):
            R_ = X if si < 2 else SA
            d = dict(
                gU=R_.take(128, F32), dec=R_.take(256), attn=R_.take(128), AX=R_.take(384), AB8=R_.take(256),
                AB1=R_.take(256), AB2=R_.take(256), W=[R_.take(256) for _ in range(2)], XY=R_.take(256),
                rhsu=R_.take(128), rhsw=R_.take(128), kdec=R_.take(128), nwT=R_.take(128), vnew=R_.take(128),
                av=R_.take(128, F32),
            )
            d["b"] = {k: self.nb(k) for k in ("gU", "dec", "attn", "B0", "A0at", "AB8", "AB1", "AB2", "W0", "W1", "XY",
                                              "rhsu", "rhsw", "kdec", "nwT", "vnew", "av")}
            tmp.append(d)
        gmb = X.take(8 * 256)
        mkb = X.take(512)
        b_gmb = self.nb("gmb")
        S.dma("pool", [(gmb, self.gm_d), (mkb, self.dm_d)], b_gmb, writes=[b_gmb])
        ident2, sb8x2 = gmb[:, 0:256], gmb[:, 256:512]
        lvmask = [gmb[:, 512 + j * 256: 768 + j * 256] for j in range(4)]
        u2x2 = (gmb[:, 6 * 256:7 * 256], gmb[:, 7 * 256:8 * 256])
        ghb = X.take(NCH * 16).rearrange("p (c k) -> p c k", k=16)
        ghf = X.take(NCH * 16, F32).rearrange("p (c k) -> p c k", k=16)
        glf = X.take(NCH * 16, F32).rearrange("p (c k) -> p c k", k=16)
        b_ghl = self.nb("ghl")
        S.op("dve", lambda e: e.tensor_copy(out=ghb, in_=gtm[:, :, 16:32]), reads=[b_gtm], writes=[b_ghl])
        S.op("dve", lambda e: e.tensor_copy(out=ghf, in_=ghb), reads=[b_ghl], writes=[b_ghl])
        S.op("dve", lambda e: e.tensor_tensor(out=glf, in0=gtm[:, :, 16:32], in1=ghf, op=ALU.subtract),
             reads=[b_gtm, b_ghl], writes=[b_ghl])
        u1b = (self.cb[:, 256:384], self.cb[:, 512:640])
        Sf = [X.take(128, F32) for _ in range(2)]
        Sb = [X.take(128) for _ in range(2)]
        b_Sf = [self.nb("Sf") for _ in range(2)]
        b_Sb = [self.nb("Sb") for _ in range(2)]
        S.op("dve", lambda e: e.memset(u, 0.0), writes=[b_u])

        for h in range(8):
            for gi, (dstT, b_dst) in enumerate(((qT, b_qT), (kT, b_kT), (vT, b_vT))):
                col = OFF_B + gi * W_B + h * 128
                wv, wb = self.load_w(wcol[:, col:col + 128], 16)
                for tt in range(4):
                    t0 = tt * 512
                    ps, b_ps = self.psb()
                    self.proj_tile(wv, wb, 16, self.hT, self.hT_bufs(t0, 512), t0, 512, ps, b_ps)
                    S.op("act", lambda e, ps=ps, t0=t0: e.copy(out=u[:, 1 + t0:1 + t0 + 512], in_=ps[:, :]),
                         reads=[b_ps], writes=[b_u])
                self.conv3(u, b_u, acc, b_acc, self.cvb[:, gi * 8 + h, :])
                if gi == 2:
                    S.op("act", lambda e: e.activation(out=vT, in_=acc, func=AF.Silu), reads=[b_acc], writes=[b_vT])
                    continue
                S.op("act", lambda e: e.activation(out=sf, in_=acc, func=AF.Silu), reads=[b_acc], writes=[b_sf])
                S.op("act", lambda e: e.activation(out=sq, in_=sf, func=AF.Square), reads=[b_sf], writes=[b_sq])
                scale = float(128 ** -0.5) if gi == 0 else 1.0
                for tt in range(4):
                    t0 = tt * 512
                    ps, b_ps = self.psb()
                    S.op("pe", lambda e, ps=ps, t0=t0: e.matmul(ps[:, :], lhsT=self.onesf, rhs=sq[:, t0:t0 + 512],
                                                                start=True, stop=True),
                         reads=[b_sq, self.b_const], writes=[b_ps])
                    S.op("act", lambda e, ps=ps: e.activation(out=rn, in_=ps[:, :], func=AF.Sqrt, bias=EPS),
                         reads=[b_ps], writes=[b_rn])
                    S.op("dve", lambda e: e.reciprocal(out=rn, in_=rn), reads=[b_rn], writes=[b_rn])
                    S.op("dve", lambda e, t0=t0, dstT=dstT, scale=scale: e.scalar_tensor_tensor(
                        out=dstT[:, t0:t0 + 512], in0=sf[:, t0:t0 + 512], scalar=scale, in1=rn, op0=ALU.mult, op1=ALU.mult),
                        reads=[b_sf, b_rn], writes=[b_dst])
            col = OFF_B + 3 * W_B + h * 128
            wv, wb = self.load_w(wcol[:, col:col + 128], 16)
            for tt in range(4):
                t0 = tt * 512
                ps, b_ps = self.psb()
                self.proj_tile(wv, wb, 16, self.hT, self.hT_bufs(t0, 512), t0, 512, ps, b_ps)
                S.op("act", lambda e, ps=ps, t0=t0: e.activation(out=sgate[:, t0:t0 + 512], in_=ps[:, :], func=AF.Silu),
                     reads=[b_ps], writes=[b_sgate])
            self.to_token_major(kT, b_kT, ktm, b_ktm)
            self.to_token_major(vT, b_vT, vtm, b_vtm)
            if BCUT == 2:
                return

            S.barrier()
            S.op("dve", lambda e: e.memset(oacc.rearrange("p c k -> p (c k)"), 0.0), writes=b_oacc)

            def stageA(d, c, tp, h=h):
                col_ = d * 8 + h
                u1, u2 = (U1, U2) if d == 0 else (U1T, U2T)
                mk = mkb[:, d * 256:(d + 1) * 256]
                tb = tp["b"]
                AX = tp["AX"]
                attnT, A0, B0 = AX[:, 0:128], AX[:, 128:256], AX[:, 256:384]
                AB0 = AX[:, 128:384]
                bAB0 = [tb["A0at"], tb["B0"]]
                cs = slice(c * 128, (c + 1) * 128)
                gUb = tp["gU"].bitcast(BF16)
                S.op("dve", lambda e: e.tensor_scalar(out=gUb[:, 0:128], in0=u1b[d], scalar1=ghf[:, c, col_:col_ + 1], scalar2=None,
                                                      op0=ALU.mult), reads=[b_ghl, self.b_const], writes=[tb["gU"]])
                S.op("dve", lambda e: e.tensor_scalar(out=gUb[:, 128:256], in0=u1b[d], scalar1=glf[:, c, col_:col_ + 1], scalar2=None,
                                                      op0=ALU.mult), reads=[b_ghl, self.b_const], writes=[tb["gU"]])
                psD, b_psD = self.psb()
                S.op("pe", lambda e: e.matmul(psD[:, 0:256], lhsT=self.identb, rhs=mk, start=True, stop=False),
                     reads=[self.b_const, b_gmb], writes=[b_psD])
                for hf in range(2):
                    S.op("pe", lambda e, hf=hf: e.matmul(psD[:, 0:256], lhsT=gUb[:, hf * 128:(hf + 1) * 128], rhs=u2x2[d],
                                                         start=False, stop=(hf == 1)),
                         reads=[tb["gU"], b_gmb], writes=[b_psD])
                S.op("act", lambda e: e.activation(out=tp["dec"], in_=psD[:, 0:256], func=AF.Exp), reads=[b_psD], writes=[tb["dec"]])
                psK, b_psK = self.psb()
                S.op("pe", lambda e: e.matmul(psK[:, 0:128], lhsT=kT[:, cs], rhs=kT[:, cs], start=True, stop=True),
                     reads=[b_kT], writes=[b_psK])
                S.op("pe", lambda e: e.matmul(psK[:, 128:256], lhsT=qT[:, cs], rhs=kT[:, cs], start=True, stop=True),
                     reads=[b_kT, b_qT], writes=[b_psK])
                S.op("act", lambda e: e.activation(out=tp["rhsu"], in_=vtm[:, c, :], func=AF.Copy, scale=btm[:, c, col_:col_ + 1]),
                     reads=[b_vtm, b_btm], writes=[tb["rhsu"]])
                S.op("act", lambda e: e.activation(out=tp["rhsw"], in_=ktm[:, c, :], func=AF.Copy, scale=begc[:, c, col_:col_ + 1]),
                     reads=[b_ktm, b_col], writes=[tb["rhsw"]])
                S.op("act", lambda e: e.activation(out=tp["kdec"], in_=ktm[:, c, :], func=AF.Copy, scale=ek[:, c, col_:col_ + 1]),
                     reads=[b_ktm, b_col], writes=[tb["kdec"]])
                yield
                S.op("dve", lambda e: e.scalar_tensor_tensor(out=B0, in0=psK[:, 0:128], scalar=nbet[:, c, col_:col_ + 1],
                                                             in1=tp["dec"][:, 0:128], op0=ALU.mult, op1=ALU.mult),
                     reads=[b_psK, b_col, tb["dec"]], writes=[tb["B0"]])
                S.op("dve", lambda e: e.tensor_tensor(out=tp["attn"], in0=psK[:, 128:256], in1=tp["dec"][:, 128:256], op=ALU.mult),
                     reads=[b_psK, tb["dec"]], writes=[tb["attn"]])
                psT, b_psT = self.psb()
                pv = psT[:, :].bitcast(BF16)
                S.op("pe", lambda e: e.transpose(out=pv[:, 0:128], in_=tp["attn"], identity=self.identb),
                     reads=[tb["attn"], self.b_const], writes=[b_psT])
                S.op("pe", lambda e: e.transpose(out=pv[:, 128:256], in_=B0, identity=self.identb),
                     reads=[tb["B0"], self.b_const], writes=[b_psT])
                S.op("act", lambda e: e.copy(out=AX[:, 0:256], in_=pv[:, 0:256]), reads=[b_psT], writes=[tb["A0at"]])
                yield
                S.op("dve", lambda e: e.tensor_tensor(out=tp["AB8"], in0=AB0, in1=sb8x2, op=ALU.mult),
                     reads=bAB0 + [b_gmb], writes=[tb["AB8"]])
                S.op("dve", lambda e: e.tensor_tensor(out=tp["W"][0], in0=tp["AB8"], in1=ident2, op=ALU.add),
                     reads=[tb["AB8"], b_gmb], writes=[tb["W0"]])
                A8, B8 = tp["AB8"][:, 0:128], tp["AB8"][:, 128:256]
                psI, b_psI = self.psb()
                S.op("pe", lambda e: e.matmul(psI[:, 0:128], lhsT=B8, rhs=A8, start=True, stop=True), reads=[tb["AB8"]], writes=[b_psI])
                S.op("pe", lambda e: e.matmul(psI[:, 128:256], lhsT=A8, rhs=B8, start=True, stop=True), reads=[tb["AB8"]], writes=[b_psI])
                S.op("act", lambda e: e.copy(out=tp["AB1"], in_=psI[:, 0:256]), reads=[b_psI], writes=[tb["AB1"]])
                yield
                A1, B1 = tp["AB1"][:, 0:128], tp["AB1"][:, 128:256]
                psI2, b_psI2 = self.psb()
                S.op("pe", lambda e: e.matmul(psI2[:, 0:128], lhsT=B1, rhs=A1, start=True, stop=True), reads=[tb["AB1"]], writes=[b_psI2])
                S.op("pe", lambda e: e.matmul(psI2[:, 128:256], lhsT=A1, rhs=B1, start=True, stop=True), reads=[tb["AB1"]], writes=[b_psI2])
                S.op("act", lambda e: e.copy(out=tp["AB2"], in_=psI2[:, 0:256]), reads=[b_psI2], writes=[tb["AB2"]])
                W0, W1 = tp["W"][0], tp["W"][1]
                psP, b_psP = self.psb()
                S.op("pe", lambda e: e.matmul(psP[:, 0:128], lhsT=B1, rhs=W0[:, 0:128], start=True, stop=True),
                     reads=[tb["AB1"], tb["W0"]], writes=[b_psP])
                S.op("pe", lambda e: e.matmul(psP[:, 128:256], lhsT=A1, rhs=W0[:, 128:256], start=True, stop=True),
                     reads=[tb["AB1"], tb["W0"]], writes=[b_psP])
                S.op("dve", lambda e: e.tensor_tensor(out=W1, in0=psP[:, 0:256], in1=W0, op=ALU.add),
                     reads=[b_psP, tb["W0"]], writes=[tb["W1"]])
                yield
                A2, B2 = tp["AB2"][:, 0:128], tp["AB2"][:, 128:256]
                psP2, b_psP2 = self.psb()
                S.op("pe", lambda e: e.matmul(psP2[:, 0:128], lhsT=B2, rhs=W1[:, 0:128], start=True, stop=True),
                     reads=[tb["AB2"], tb["W1"]], writes=[b_psP2])
                S.op("pe", lambda e: e.matmul(psP2[:, 128:256], lhsT=A2, rhs=W1[:, 128:256], start=True, stop=True),
                     reads=[tb["AB2"], tb["W1"]], writes=[b_psP2])
                S.op("dve", lambda e: e.tensor_tensor(out=W0, in0=psP2[:, 0:256], in1=W1, op=ALU.add),
                     reads=[b_psP2, tb["W1"]], writes=[tb["W0"]])
                yield
                Wc, bWc, Wn, bWn = W0, tb["W0"], W1, tb["W1"]
                for lv in range(4):
                    TTc, Tc = Wc[:, 0:128], Wc[:, 128:256]
                    psX, b_psX = self.psb()
                    S.op("pe", lambda e, psX=psX, TTc=TTc: e.matmul(psX[:, 0:128], lhsT=B0, rhs=TTc, start=True, stop=True),
                         reads=[tb["B0"], bWc], writes=[b_psX])
                    S.op("pe", lambda e, psX=psX, Tc=Tc: e.matmul(psX[:, 128:256], lhsT=A0, rhs=Tc, start=True, stop=True),
                         reads=[tb["A0at"], bWc], writes=[b_psX])
                    S.op("dve", lambda e, psX=psX, lv=lv: e.tensor_tensor(out=tp["XY"], in0=psX[:, 0:256], in1=lvmask[lv], op=ALU.mult),
                         reads=[b_psX, b_gmb], writes=[tb["XY"]])
                    yield
                    psZ, b_psZ = self.psb()
                    S.op("pe", lambda e, psZ=psZ, Wc=Wc: e.matmul(psZ[:, 0:256], lhsT=self.identb, rhs=Wc, start=True, stop=False),
                         reads=[self.b_const, bWc], writes=[b_psZ])
                    S.op("pe", lambda e, psZ=psZ, Tc=Tc: e.matmul(psZ[:, 0:128], lhsT=Tc, rhs=tp["XY"][:, 0:128], start=False, stop=False),
                         reads=[bWc, tb["XY"]], writes=[b_psZ])
                    S.op("pe", lambda e, psZ=psZ, TTc=TTc: e.matmul(psZ[:, 128:256], lhsT=TTc, rhs=tp["XY"][:, 128:256], start=False, stop=True),
                         reads=[bWc, tb["XY"]], writes=[b_psZ])
                    S.op("act", lambda e, psZ=psZ, Wn=Wn: e.copy(out=Wn, in_=psZ[:, 0:256]), reads=[b_psZ], writes=[bWn])
                    Wc, bWc, Wn, bWn = Wn, bWn, Wc, bWc
                    yield
                TT, bTT = Wc[:, 0:128], bWc
                psW, b_psW = self.psb()
                S.op("pe", lambda e: e.matmul(psW[:, 0:128], lhsT=tp["rhsw"], rhs=TT, start=True, stop=True),
                     reads=[tb["rhsw"], bTT], writes=[b_psW])
                S.op("act", lambda e: e.mul(out=tp["nwT"], in_=psW[:, 0:128], mul=-1.0), reads=[b_psW], writes=[tb["nwT"]])
                yield

            def tail(d, c, tp, h=h):
                col_ = d * 8 + h
                tb = tp["b"]
                cs = slice(c * 128, (c + 1) * 128)
                attnT = tp["AX"][:, 0:128]
                TT, bTT = tp["W"][0][:, 0:128], tb["W0"]
                psV, b_psV = self.psb()
                S.op("pe", lambda e: e.matmul(psV[:, 0:128], lhsT=TT, rhs=tp["rhsu"], start=True, stop=False),
                     reads=[tb["rhsu"], bTT], writes=[b_psV])
                S.op("pe", lambda e: e.matmul(psV[:, 0:128], lhsT=tp["nwT"], rhs=Sb[d], start=False, stop=True),
                     reads=[tb["nwT"], b_Sb[d]], writes=[b_psV])
                S.op("act", lambda e: e.copy(out=tp["vnew"], in_=psV[:, 0:128]), reads=[b_psV], writes=[tb["vnew"]])
                psO, b_psO = self.psb()
                S.op("pe", lambda e: e.matmul(psO[:, 0:128], lhsT=qT[:, cs], rhs=Sb[d], start=True, stop=True),
                     reads=[b_qT, b_Sb[d]], writes=[b_psO])
                yield
                S.op("pe", lambda e: e.matmul(psO[:, 128:256], lhsT=attnT, rhs=tp["vnew"], start=True, stop=True),
                     reads=[tb["A0at"], tb["vnew"]], writes=[b_psO])
                psS, b_psS = self.psb()
                S.op("pe", lambda e: e.matmul(psS[:, 0:128], lhsT=tp["kdec"], rhs=tp["vnew"], start=True, stop=True),
                     reads=[tb["kdec"], tb["vnew"]], writes=[b_psS])
                S.op("dve", lambda e: e.scalar_tensor_tensor(out=Sf[d], in0=Sf[d], scalar=egl[:, c, col_:col_ + 1], in1=psS[:, 0:128],
                                                             op0=ALU.mult, op1=ALU.add),
                     reads=[b_psS, b_col, b_Sf[d]], writes=[b_Sf[d]])
                S.op("act", lambda e: e.copy(out=Sb[d], in_=Sf[d]), reads=[b_Sf[d]], writes=[b_Sb[d]])
                S.op("dve", lambda e: e.tensor_tensor(out=tp["av"], in0=psO[:, 128:256], in1=oacc[:, c, :], op=ALU.add),
                     reads=[b_psO, b_oacc[c]], writes=[tb["av"]])
                S.op("dve", lambda e: e.scalar_tensor_tensor(out=oacc[:, c, :], in0=psO[:, 0:128], scalar=egc[:, c, col_:col_ + 1],
                                                             in1=tp["av"], op0=ALU.mult, op1=ALU.add),
                     reads=[b_psO, b_col, tb["av"]], writes=[b_oacc[c]])
                yield

            def scan(d):
                S.op("dve", lambda e: e.memset(Sf[d], 0.0), writes=[b_Sf[d]])
                S.op("dve", lambda e: e.memset(Sb[d], 0.0), writes=[b_Sb[d]])
                chunks = list(range(NCH)) if d == 0 else list(range(NCH - 1, -1, -1))
                sets = (tmp[d], tmp[2 + d])
                prev = None
                for i, c in enumerate(chunks):
                    gs = [stageA(d, c, sets[i % 2])] + ([prev] if prev is not None else [])
                    while gs:
                        for g in list(gs):
                            try:
                                next(g)
                            except StopIteration:
                                gs.remove(g)
                            yield
                    prev = tail(d, c, sets[i % 2])
                for _ in prev:
                    yield

            gens = [scan(0), scan(1)]
            while gens:
                for g in list(gens):
                    try:
                        next(g)
                    except StopIteration:
                        gens.remove(g)
            S.barrier()

            S.op("dve", lambda e: e.tensor_tensor(out=sq.rearrange("p (c k) -> p c k", k=128), in0=oacc, in1=oacc, op=ALU.mult),
                 reads=b_oacc, writes=[b_sq])
            S.op("dve", lambda e: e.tensor_reduce(out=ms16, in_=sq.rearrange("p (c k) -> p c k", k=128), axis=AX.X, op=ALU.add),
                 reads=[b_sq], writes=[b_ms16])
            S.op("act", lambda e: e.activation(out=ms16, in_=ms16, func=AF.Sqrt, scale=1.0 / 128, bias=EPS),
                 reads=[b_ms16], writes=[b_ms16])
            S.op("dve", lambda e: e.reciprocal(out=ms16, in_=ms16), reads=[b_ms16], writes=[b_ms16])
            S.op("dve", lambda e: e.tensor_tensor(out=oacc, in0=oacc, in1=ms16.unsqueeze(2).broadcast_to([128, NCH, 128]),
                                                  op=ALU.mult), reads=b_oacc + [b_ms16], writes=b_oacc)
            S.op("dve", lambda e: e.tensor_tensor(out=onb, in0=oacc, in1=self.gnwb.unsqueeze(1).broadcast_to([128, NCH, 128]),
                                                  op=ALU.mult), reads=b_oacc + [self.b_lp], writes=[b_onb])
            for half in range(2):
                ps, b_ps = self.psb()
                pv = ps[:, :].bitcast(BF16)
                for k in range(8):
                    c = half * 8 + k
                    S.op("pe", lambda e, pv=pv, k=k, c=c: e.transpose(out=pv[:, k * 128:(k + 1) * 128], in_=onb[:, c, :],
                                                                       identity=self.identb),
                         reads=[b_onb, self.b_const], writes=[b_ps])
                S.op("dve", lambda e, pv=pv, half=half: e.tensor_tensor(out=ybT[:, half * 1024:(half + 1) * 1024], in0=pv,
                                                                         in1=sgate[:, half * 1024:(half + 1) * 1024], op=ALU.mult),
                     reads=[b_ps, b_sgate], writes=[b_ybT])
            S.dma("sp", [(self.yscr[1, h], ybT)], b_ybT, reads=[b_ybT], writes=[self.b_yscr[1][h]])

    def phase_C(self, l):
        S = self.S
        X = self.X
        self.begin()
        wcol = self.w_in[l]
        qT, kT, vT, sgate = X.take(T), X.take(T), X.take(T), X.take(T)
        b_qT, b_kT, b_vT, b_sgate = self.nb("qT"), self.nb("kT"), self.nb("vT"), self.nb("sgate")
        vtm = X.take(T).rearrange("p (c k) -> p c k", k=128)
        b_vtm = self.nb("vtm")
        bias32 = X.take(NBLK * 128, F32)
        bm = X.take(NBLK * 128)
        b_bias32, b_bm = self.nb("bias32"), self.nb("bm")
        pT = [X.take(640) for _ in range(2)]
        b_pT = [self.nb("pT") for _ in range(2)]
        rden = [X.take(512, F32) for _ in range(2)]
        b_rden = [self.nb("rden") for _ in range(2)]
        on = [X.take(512, F32) for _ in range(2)]
        b_on = [self.nb("on") for _ in range(2)]
        ycT = [X.take(T) for _ in range(2)]
        b_ycT = [self.nb("ycT") for _ in range(2)]
        amf = X.take(NBLK * 128, F32)
        b_amf = self.nb("amf")
        S.op("dve", lambda e: e.tensor_copy(out=amf, in_=self.amask), reads=[self.b_const], writes=[b_amf])
        for h in range(8):
            S.dma("sp", [(bias32, self.biasg[l, h])], b_bias32, writes=[b_bias32])
            S.op("dve", lambda e: e.tensor_tensor(out=bm, in0=bias32, in1=amf, op=ALU.add),
                 reads=[b_bias32, b_amf], writes=[b_bm])
            for gi, (dstT, b_dst) in enumerate(((qT, b_qT), (kT, b_kT), (vT, b_vT), (sgate, b_sgate))):
                col = OFF_C + gi * W_C + h * 128
                wv, wb = self.load_w(wcol[:, col:col + 128], 16)
                for tt in range(4):
                    t0 = tt * 512
                    ps, b_ps = self.psb()
                    self.proj_tile(wv, wb, 16, self.hT, self.hT_bufs(t0, 512), t0, 512, ps, b_ps)
                    if gi == 0:
                        S.op("act", lambda e, ps=ps, t0=t0, dstT=dstT: e.mul(out=dstT[:, t0:t0 + 512], in_=ps[:, :],
                                                                              mul=float(128 ** -0.5)),
                             reads=[b_ps], writes=[b_dst])
                    elif gi == 3:
                        S.op("act", lambda e, ps=ps, t0=t0, dstT=dstT: e.activation(out=dstT[:, t0:t0 + 512], in_=ps[:, :],
                                                                                     func=AF.Silu), reads=[b_ps], writes=[b_dst])
                    else:
                        S.op("act", lambda e, ps=ps, t0=t0, dstT=dstT: e.copy(out=dstT[:, t0:t0 + 512], in_=ps[:, :]),
                             reads=[b_ps], writes=[b_dst])
            self.to_token_major(vT, b_vT, vtm, b_vtm)
            yc = ycT[h % 2]
            b_yc = b_ycT[h % 2]
            for n4 in range(4):
                psO, b_psO = self.psb(0, 4)
                psDn, b_psDn = self.psb(0, 4)
                i4 = n4 % 2
                for nn in range(4):
                    n = n4 * 4 + nn
                    kts = att_keytiles(n)
                    i2 = n % 2
                    psA, b_psA = self.ps[4 + 2 * i2], self.b_ps[4 + 2 * i2]
                    psB, b_psB = self.ps[5 + 2 * i2], self.b_ps[5 + 2 * i2]
                    for b, m in enumerate(kts):
                        blk = att_block_id(n, m)
                        pp, bpp = (psA, b_psA) if b < 4 else (psB, b_psB)
                        o0 = (b % 4) * 128
                        S.op("pe", lambda e, pp=pp, o0=o0, blk=blk: e.matmul(pp[:, o0:o0 + 128], lhsT=self.identb,
                                                                               rhs=bm[:, blk * 128:(blk + 1) * 128],
                                                                               start=True, stop=False),
                             reads=[b_bm, self.b_const], writes=[bpp])
                        S.op("pe", lambda e, pp=pp, o0=o0, m=m, n=n: e.matmul(pp[:, o0:o0 + 128], lhsT=kT[:, m * 128:(m + 1) * 128],
                                                                                rhs=qT[:, n * 128:(n + 1) * 128],
                                                                                start=False, stop=True),
                             reads=[b_kT, b_qT], writes=[bpp])
                    S.op("act", lambda e, psA=psA, i2=i2: e.activation(out=pT[i2][:, 0:512], in_=psA[:, :], func=AF.Exp),
                         reads=[b_psA], writes=[b_pT[i2]])
                    if len(kts) == 5:
                        S.op("act", lambda e, psB=psB, i2=i2: e.activation(out=pT[i2][:, 512:640], in_=psB[:, 0:128], func=AF.Exp),
                             reads=[b_psB], writes=[b_pT[i2]])
                    for b, m in enumerate(kts):
                        S.op("pe", lambda e, psO=psO, nn=nn, m=m, b=b, i2=i2, nk=len(kts): e.matmul(
                            psO[:, nn * 128:(nn + 1) * 128], lhsT=vtm[:, m, :], rhs=pT[i2][:, b * 128:(b + 1) * 128],
                            start=(b == 0), stop=(b == nk - 1)), reads=[b_vtm, b_pT[i2]], writes=[b_psO])
                    for b, m in enumerate(kts):
                        S.op("pe", lambda e, psDn=psDn, nn=nn, b=b, i2=i2, nk=len(kts): e.matmul(
                            psDn[:, nn * 128:(nn + 1) * 128], lhsT=self.onesb, rhs=pT[i2][:, b * 128:(b + 1) * 128],
                            start=(b == 0), stop=(b == nk - 1)), reads=[self.b_const, b_pT[i2]], writes=[b_psDn])
                t0 = n4 * 512
                S.op("dve", lambda e, psDn=psDn, i4=i4: e.reciprocal(out=rden[i4], in_=psDn[:, :]), reads=[b_psDn], writes=[b_rden[i4]])
                S.op("dve", lambda e, psO=psO, i4=i4: e.tensor_tensor(out=on[i4], in0=psO[:, :], in1=rden[i4], op=ALU.mult),
                     reads=[b_psO, b_rden[i4]], writes=[b_on[i4]])
                S.op("dve", lambda e, i4=i4, t0=t0, yc=yc: e.tensor_tensor(out=yc[:, t0:t0 + 512], in0=on[i4],
                                                                            in1=sgate[:, t0:t0 + 512], op=ALU.mult),
                     reads=[b_on[i4], b_sgate], writes=[b_yc])
            S.dma("sp", [(self.yscr[2, h], yc)], b_yc, reads=[b_yc], writes=[self.b_yscr[2][h]])

    def phase_M(self, l, src, b_src_rows, dst, b_dst_rows):
        S = self.S
        X = self.X
        self.begin()
        HALF = 1024
        yt = [X.take(8 * HALF).rearrange("p (c t) -> p c t", c=8) for _ in range(3)]
        b_yt = [self.nb(f"yt{b}") for b in range(3)]
        mT = X.take(16 * HALF).rearrange("p (k t) -> p k t", k=16)
        b_mT = [self.nb(f"mT{i}") for i in range(HALF // 128)]
        sig = [X.take(512, F32) for _ in range(2)]
        b_sig = [self.nb("sig") for _ in range(2)]
        macc = X.take(512, F32)
        b_macc = self.nb("macc")
        mtmp = X.take(512, F32)
        b_mtmp = self.nb("mtmp")
        xb = [X.take(512, F32) for _ in range(2)]
        b_xb = [self.nb("xb") for _ in range(2)]
        ob = [X.take(512, F32) for _ in range(2)]
        b_ob = [self.nb("ob") for _ in range(2)]
        wps = (self.w_pa[l], self.w_pb[l], self.w_pc[l])
        wcol = self.w_in[l]
        for th in range(2):
            tb0 = th * HALF
            for br in range(3):
                S.dma("sp", [(yt[br], self.yscr[br].rearrange("c p t -> p c t")[:, :, tb0:tb0 + HALF])], b_yt[br],
                      reads=self.b_yscr[br], writes=[b_yt[br]])
            for dmc in range(16):
                wg = [self.load_w(wcol[:, OFF_G + br * D + dmc * 128: OFF_G + br * D + (dmc + 1) * 128], 16) for br in range(3)]
                wp = [self.load_w(wps[br][:, dmc * 128:(dmc + 1) * 128], 8) for br in range(3)]
                for tt in range(2):
                    t0 = tb0 + tt * 512
                    for br in range(3):
                        psg, b_psg = self.psb()
                        psp, b_psp = self.psb()
                        self.proj_tile(wg[br][0], wg[br][1], 16, self.hT, self.hT_bufs(t0, 512), t0, 512, psg, b_psg)
                        self.proj_tile(wp[br][0], wp[br][1], 8, yt[br], [b_yt[br]], tt * 512, 512, psp, b_psp)
                        i = br % 2
                        S.op("act", lambda e, psg=psg, i=i: e.activation(out=sig[i], in_=psg[:, :], func=AF.Sigmoid),
                             reads=[b_psg], writes=[b_sig[i]])
                        mdst = mT[:, dmc, tt * 512:(tt + 1) * 512]
                        bmd = b_mT[tt * 4:(tt + 1) * 4]
                        if br == 0:
                            S.op("dve", lambda e, psp=psp, i=i: e.tensor_tensor(out=macc, in0=psp[:, :], in1=sig[i], op=ALU.mult),
                                 reads=[b_psp, b_sig[i]], writes=[b_macc])
                        else:
                            S.op("dve", lambda e, psp=psp, i=i: e.tensor_tensor(out=mtmp, in0=psp[:, :], in1=sig[i], op=ALU.mult),
                                 reads=[b_psp, b_sig[i]], writes=[b_mtmp])
                            if br == 1:
                                S.op("dve", lambda e: e.tensor_tensor(out=macc, in0=macc, in1=mtmp, op=ALU.add),
                                     reads=[b_macc, b_mtmp], writes=[b_macc])
                            else:
                                S.op("dve", lambda e, mdst=mdst: e.tensor_tensor(out=mdst, in0=macc, in1=mtmp, op=ALU.add),
                                     reads=[b_macc, b_mtmp], writes=bmd)
            for dq in range(4):
                i0, bufs = self.wslot(4)
                wov = self.ring[:, i0:i0 + 4, :].rearrange("p s e -> p (s e)").rearrange("p (k e) -> p k e", k=16)
                S.dma("pool", [(wov, self.w_o[l][:, dq * 512:(dq + 1) * 512].rearrange("(k p) e -> p k e", p=128))],
                      bufs[0], writes=bufs)
                for tcl in range(HALF // 128):
                    tc = th * (HALF // 128) + tcl
                    ps, b_ps = self.psb()
                    for k in range(16):
                        S.op("pe", lambda e, ps=ps, k=k, tcl=tcl, wov=wov: e.matmul(
                            ps[:, :], lhsT=mT[:, k, tcl * 128:(tcl + 1) * 128], rhs=wov[:, k, :], start=(k == 0), stop=(k == 15)),
                            reads=list(bufs) + [b_mT[tcl]], writes=[b_ps])
                    i = (dq * 8 + tcl) % 2
                    rows = slice(tc * 128, (tc + 1) * 128)
                    cols = slice(dq * 512, (dq + 1) * 512)
                    S.dma("sp", [(xb[i], src[rows, cols])], b_xb[i], reads=[b_src_rows[tc]], writes=[b_xb[i]])
                    S.op("dve", lambda e, ps=ps, i=i: e.tensor_tensor(out=ob[i], in0=ps[:, :], in1=xb[i], op=ALU.add),
                         reads=[b_ps, b_xb[i]], writes=[b_ob[i]])
                    S.dma("sp", [(dst[rows, cols], ob[i])], b_ob[i], reads=[b_ob[i]], writes=[b_dst_rows[tc]])

    def phase_F(self, src, b_src_rows, dst):
        S = self.S
        X = self.X
        self.begin()
        fwb = X.take(D, F32)
        b_fwb = self.nb("fwb")
        xt = [X.take(D, F32) for _ in range(2)]
        ot = [X.take(D, F32) for _ in range(2)]
        junk = X.take(D)
        ss = [X.take(1, F32) for _ in range(2)]
        b_xt = [self.nb("xt") for _ in range(2)]
        b_ot = [self.nb("ot") for _ in range(2)]
        b_ss = [self.nb("ss") for _ in range(2)]
        b_junk = self.nb("junk")
        S.dma("sp", [(fwb, self.fnw.partition_broadcast(128))], b_fwb, writes=[b_fwb])
        for tc in range(NCH):
            i = tc % 2
            rows = slice(tc * 128, (tc + 1) * 128)
            S.dma("sp", [(xt[i], src[rows, :])], b_xt[i], reads=[b_src_rows[tc]], writes=[b_xt[i]])
            S.op("act", lambda e, i=i: e.activation(out=junk, in_=xt[i], func=AF.Square, scale=float(D) ** -0.5,
                                                    accum_out=ss[i]), reads=[b_xt[i]], writes=[b_junk, b_ss[i]])
            S.op("act", lambda e, i=i: e.activation(out=ss[i], in_=ss[i], func=AF.Sqrt, bias=EPS),
                 reads=[b_ss[i]], writes=[b_ss[i]])
            S.op("dve", lambda e, i=i: e.reciprocal(out=ss[i], in_=ss[i]), reads=[b_ss[i]], writes=[b_ss[i]])
            S.op("dve", lambda e, i=i: e.scalar_tensor_tensor(out=ot[i], in0=xt[i], scalar=ss[i][:, 0:1], in1=fwb,
                                                              op0=ALU.mult, op1=ALU.mult),
                 reads=[b_xt[i], b_ss[i], b_fwb], writes=[b_ot[i]])
            S.dma("sp", [(dst[rows, :], ot[i])], b_ot[i], reads=[b_ot[i]], writes=[self.b_y])

    def build(self, phases="NABCMF"):
        S = self.S
        self.load_consts()
        for s in range(self.nseq):
            b_xin = [Buf("xin") for _ in range(NCH)]
            cur, b_cur = self.x[s], b_xin
            for l in range(self.nl):
                self.load_layer_params(l)
                nxt, b_nxt = self.xs[l % 2], self.b_xs[l % 2]
                if "N" in phases:
                    self.phase_norm(cur, b_cur, l)
                if "A" in phases:
                    self.phase_A(l)
                if "B" in phases:
                    self.phase_B(l)
                if "C" in phases:
                    self.phase_C(l)
                if "M" in phases:
                    self.phase_M(l, cur, b_cur, nxt, b_nxt)
                    cur, b_cur = nxt, b_nxt
            if "F" in phases:
                self.phase_F(cur, b_cur, self.y[s])
        S.barrier()
        self.stats = S.emit()
        return self.nc


NCORES = 8
_CACHE = {}


def kernel(x_prompt, x_sample, norm_w, w_in, conv_a, conv_b, a_log, dt_bias, gdn_norm_w, na_bias,
           w_pa, w_pb, w_pc, w_o, final_norm_w):
    f = lambda a: np.ascontiguousarray(np.asarray(a, dtype=np.float32))
    xa = np.concatenate([f(x_prompt), f(x_sample)], axis=0)
    nseq = xa.shape[0] // NCORES
    cf, dm, amask = host_consts()
    shared = {
        "norm_w": f(norm_w), "w_in": f(w_in), "conv_a": f(conv_a), "conv_b": f(conv_b),
        "a_log": f(a_log).reshape(2, 16), "dt_bias": f(dt_bias).reshape(2, 16), "gdn_norm_w": f(gdn_norm_w),
        "biasg": gather_bias(f(na_bias)), "w_pa": f(w_pa), "w_pb": f(w_pb), "w_pc": f(w_pc), "w_o": f(w_o),
        "final_norm_w": f(final_norm_w), "cf": cf, "dm": dm, "amask": amask, "gm": host_gm(),
    }
    nc = Builder(nseq).build()
    in_maps = []
    for c in range(NCORES):
        m = dict(shared)
        m["x"] = np.ascontiguousarray(xa[c * nseq:(c + 1) * nseq])
        in_maps.append(m)
    res = run_bass_kernel_spmd(nc, in_maps, core_ids=list(range(NCORES)))
    ys = np.concatenate([np.asarray(r["y"], dtype=np.float32) for r in res.results], axis=0)
    nb = x_prompt.shape[0]
    return (ys[:nb], ys[nb:])
```
